# Optimizing a Trainium2 kernel written in Bass

```python
import jax, jax.numpy as jnp
from jax import lax
import numpy as np

D_MODEL = 1024
BATCH = 8
SEQ = 4096
DEPTH = 2
DEC_BATCH = 4
DEC_SEQ = 4096
PAST_LEN = 128

GRID_W = 64
D_MIX = D_MODEL
D_RWKV = D_MIX // 2
D_NAT = D_MIX - D_RWKV
HEAD_DIM = 64
H_RWKV = D_RWKV // HEAD_DIM
H_NAT = D_NAT // HEAD_DIM
N_DIR = 2
DECAY_LORA = 64
AAA_LORA = 64
GATE_LORA = 128
MV_LORA = 32
WIN_ROWS = 8
WIN_COLS = 16
D_FF = 2816
RMS_EPS = 1e-6
LNX_EPS = 64e-5
RW_COLS = 3 * D_RWKV + N_DIR * DECAY_LORA + N_DIR * AAA_LORA + GATE_LORA
IN_COLS = RW_COLS + 3 * D_NAT

kernel_name = 'hybrid_rwkv7_natten2d_encoder'


def rms_norm(x, g):
    xf = x.astype(jnp.float32)
    y = xf * lax.rsqrt(jnp.mean(xf * xf, axis=-1, keepdims=True) + RMS_EPS)
    return (y * g.astype(jnp.float32)).astype(x.dtype)


def shift_prev(x):
    return jnp.pad(x, ((0, 0), (1, 0), (0, 0)))[:, :-1]


def shift_next(x):
    return jnp.pad(x, ((0, 0), (0, 1), (0, 0)))[:, 1:]


def to_scan_layout(x):
    xt = jnp.transpose(x, (1, 2, 0, 3, 4))
    return jnp.stack([xt[:, 0], jnp.flip(xt[:, 1], axis=0)], axis=1)


def rwkv7_scan(r, w, k, v, kk, a):
    _, nd, b, h, n = r.shape

    def step(S, inp):
        r_t, w_t, k_t, v_t, kk_t, a_t = inp
        s_kk = jnp.einsum('dbhij,dbhj->dbhi', S, -kk_t)
        S = (S * w_t[..., None, :]
             + s_kk[..., :, None] * (kk_t * a_t)[..., None, :]
             + v_t[..., :, None] * k_t[..., None, :])
        return S, jnp.einsum('dbhij,dbhj->dbhi', S, r_t)

    S0 = jnp.zeros((nd, b, h, n, n), jnp.float32)
    _, ys = lax.scan(step, S0, (r, w, k, v, kk, a))
    return ys[:, 0] + jnp.flip(ys[:, 1], axis=0)


def rwkv7_time_mix(p, v_first, li, prm):
    f32 = jnp.float32
    B, L, _ = p.shape
    H, N = H_RWKV, HEAD_DIM
    o1 = 3 * D_RWKV
    o2 = o1 + N_DIR * DECAY_LORA
    o3 = o2 + N_DIR * AAA_LORA
    r = p[..., :D_RWKV].astype(f32)
    k = p[..., D_RWKV:2 * D_RWKV].astype(f32)
    v = p[..., 2 * D_RWKV:o1].astype(f32)
    wd = p[..., o1:o2].reshape(B, L, N_DIR, DECAY_LORA)
    ad = p[..., o2:o3].reshape(B, L, N_DIR, AAA_LORA)
    gd = p[..., o3:]
    w_raw = prm['decay_w0'][li] + jnp.einsum('bldr,drc->bldc', jnp.tanh(wd), prm['decay_up'][li])
    decay = jnp.exp(-jnp.exp(-jax.nn.softplus(-w_raw.astype(f32)) - 0.5))
    a = jax.nn.sigmoid((prm['aaa_a0'][li] + jnp.einsum('bldr,drc->bldc', ad, prm['aaa_up'][li])).astype(f32))
    g = (jax.nn.sigmoid(gd) @ prm['gate_up'][li]).astype(f32)
    if v_first is None:
        v_first = v
    else:
        j = li - 1
        mix = jax.nn.sigmoid((prm['vres_v0'][j] + (v @ prm['vres_down'][j]) @ prm['vres_up'][j]).astype(f32))
        v = v + (v_first - v) * mix
    kk = (k * prm['k_k'][li]).reshape(B, L, H, N)
    kk = kk * lax.rsqrt(jnp.maximum(jnp.sum(kk * kk, axis=-1, keepdims=True), 1e-12))
    kd = k[:, :, None, :] * (1.0 + (a - 1.0) * prm['k_a'][li])
    rh = r.reshape(B, L, H, N)
    vh = v.reshape(B, L, H, N)

    def heads(t):
        return t.reshape(B, L, N_DIR, H, N)

    def both(t):
        return jnp.broadcast_to(t[:, :, None], (B, L, N_DIR, H, N))

    y = rwkv7_scan(to_scan_layout(both(rh)), to_scan_layout(heads(decay)), to_scan_layout(heads(kd)),
                   to_scan_layout(both(vh)), to_scan_layout(both(kk)), to_scan_layout(heads(a)))
    y = jnp.transpose(y, (1, 0, 2, 3))
    mu = jnp.mean(y, axis=-1, keepdims=True)
    var = jnp.mean(jnp.square(y - mu), axis=-1, keepdims=True)
    y = ((y - mu) * lax.rsqrt(var + LNX_EPS)).reshape(B, L, D_RWKV) * prm['lnx_g'][li] + prm['lnx_b'][li]
    bonus = jnp.sum(rh * jnp.sum(heads(kd), axis=2) * prm['r_k'][li], axis=-1, keepdims=True) * vh
    out = (y + bonus.reshape(B, L, D_RWKV)) * g
    return out.astype(p.dtype), v_first


def neighbourhood_attention(q, k, v, li, prm):
    B, L, _ = q.shape
    rows = L // GRID_W
    kh = min(WIN_ROWS, rows)
    kw = WIN_COLS
    q = rms_norm(q.reshape(B, L, H_NAT, HEAD_DIM), prm['qn_g'][li]) * (HEAD_DIM ** -0.5)
    k = rms_norm(k.reshape(B, L, H_NAT, HEAD_DIM), prm['kn_g'][li])
    q_grid = q.reshape(B, rows, GRID_W, H_NAT, HEAD_DIM)
    k_grid = k.reshape(B, rows, GRID_W, H_NAT, HEAD_DIM)
    v_grid = v.reshape(B, rows, GRID_W, H_NAT, HEAD_DIM)
    cols = np.arange(GRID_W)
    col_start = np.clip(cols - kw // 2, 0, GRID_W - kw)
    col_idx = col_start[:, None] + np.arange(kw)[None, :]
    col_bias_idx = col_idx - cols[:, None] + (WIN_COLS - 1)
    rpb_cols = prm['rpb'][li][:, :, col_bias_idx].astype(jnp.float32)

    def row_block(r):
        rs = jnp.clip(r - kh // 2, 0, rows - kh)
        k_rows = lax.dynamic_slice_in_dim(k_grid, rs, kh, axis=1)
        v_rows = lax.dynamic_slice_in_dim(v_grid, rs, kh, axis=1)
        k_g = k_rows[:, :, col_idx]
        v_g = v_rows[:, :, col_idx]
        q_r = lax.dynamic_index_in_dim(q_grid, r, axis=1, keepdims=False)
        s = jnp.einsum('bchd,bicjhd->bhcij', q_r, k_g, preferred_element_type=jnp.float32)
        row_bias_idx = rs + jnp.arange(kh) - r + (WIN_ROWS - 1)
        bias = jnp.take(rpb_cols, row_bias_idx, axis=1)
        s = s + jnp.transpose(bias, (0, 2, 1, 3))[None]
        pr = jax.nn.softmax(s.reshape(B, H_NAT, GRID_W, kh * kw), axis=-1)
        pr = pr.reshape(B, H_NAT, GRID_W, kh, kw).astype(v_g.dtype)
        return jnp.einsum('bhcij,bicjhd->bchd', pr, v_g)

    out = lax.map(row_block, jnp.arange(rows))
    return jnp.transpose(out, (1, 0, 2, 3, 4)).reshape(B, L, D_NAT)


def conv_ffn(x, li, prm):
    h = rms_norm(x, prm['norm2_g'][li])
    u = h @ prm['w_up'][li]
    cw = prm['conv_w'][li]
    u = cw[0] * shift_prev(u) + cw[1] * u + cw[2] * shift_next(u) + prm['conv_b'][li]
    gate, val = u[..., :D_FF], u[..., D_FF:]
    return (jax.nn.silu(gate) * val) @ prm['w_down'][li]


def encoder_trunk(x, prm):
    v_first = None
    for li in range(DEPTH):
        h = rms_norm(x, prm['norm1_g'][li])
        proj = h @ prm['w_in'][li]
        p_rw = proj[..., :RW_COLS]
        p_na = proj[..., RW_COLS:]
        mu = prm['shift_mu'][li]
        p_rw = p_rw + mu[0] * (shift_prev(p_rw) - p_rw) + mu[1] * (shift_next(p_rw) - p_rw)
        rw_out, v_first = rwkv7_time_mix(p_rw, v_first, li, prm)
        na_out = neighbourhood_attention(p_na[..., :D_NAT], p_na[..., D_NAT:2 * D_NAT],
                                         p_na[..., 2 * D_NAT:], li, prm)
        x = x + jnp.concatenate([rw_out, na_out], axis=-1) @ prm['w_out'][li]
        x = x + conv_ffn(x, li, prm)
    return x


def setup_inputs(seed: int = 0) -> dict:
    key = jax.random.key(seed)
    ks = jax.random.split(key, 28)
    f32 = jnp.float32

    def nrm(k, shape, s):
        return s * jax.random.normal(k, shape, f32)

    def unif(k, shape, lo, hi):
        return jax.random.uniform(k, shape, f32, lo, hi)

    return {
        'x_prompt': nrm(ks[0], (BATCH, SEQ, D_MODEL), 1.0),
        'x_sample': nrm(ks[1], (DEC_BATCH, DEC_SEQ, D_MODEL), 1.0),
        'norm1_g': 1.0 + nrm(ks[2], (DEPTH, D_MODEL), 0.02),
        'w_in': nrm(ks[3], (DEPTH, D_MODEL, IN_COLS), D_MODEL ** -0.5),
        'shift_mu': unif(ks[4], (DEPTH, 2, RW_COLS), 0.0, 0.6),
        'decay_w0': unif(ks[5], (DEPTH, N_DIR, D_RWKV), -5.0, 0.5),
        'decay_up': nrm(ks[6], (DEPTH, N_DIR, DECAY_LORA, D_RWKV), 0.5 * DECAY_LORA ** -0.5),
        'aaa_a0': nrm(ks[7], (DEPTH, N_DIR, D_RWKV), 0.1),
        'aaa_up': nrm(ks[8], (DEPTH, N_DIR, AAA_LORA, D_RWKV), 0.5 * AAA_LORA ** -0.5),
        'gate_up': nrm(ks[9], (DEPTH, GATE_LORA, D_RWKV), GATE_LORA ** -0.5),
        'vres_v0': 0.5 + nrm(ks[10], (DEPTH - 1, D_RWKV), 0.1),
        'vres_down': nrm(ks[11], (DEPTH - 1, D_RWKV, MV_LORA), D_RWKV ** -0.5),
        'vres_up': nrm(ks[12], (DEPTH - 1, MV_LORA, D_RWKV), 0.5 * MV_LORA ** -0.5),
        'k_k': 0.85 + nrm(ks[13], (DEPTH, D_RWKV), 0.02),
        'k_a': 1.0 + nrm(ks[14], (DEPTH, D_RWKV), 0.02),
        'r_k': nrm(ks[15], (DEPTH, H_RWKV, HEAD_DIM), 0.1),
        'lnx_g': 1.0 + nrm(ks[16], (DEPTH, D_RWKV), 0.02),
        'lnx_b': nrm(ks[17], (DEPTH, D_RWKV), 0.02),
        'qn_g': 1.0 + nrm(ks[18], (DEPTH, HEAD_DIM), 0.02),
        'kn_g': 1.0 + nrm(ks[19], (DEPTH, HEAD_DIM), 0.02),
        'rpb': nrm(ks[20], (DEPTH, H_NAT, 2 * WIN_ROWS - 1, 2 * WIN_COLS - 1), 0.02),
        'w_out': nrm(ks[21], (DEPTH, D_MIX, D_MODEL), D_MIX ** -0.5),
        'norm2_g': 1.0 + nrm(ks[22], (DEPTH, D_MODEL), 0.02),
        'w_up': nrm(ks[23], (DEPTH, D_MODEL, 2 * D_FF), D_MODEL ** -0.5),
        'conv_w': nrm(ks[24], (DEPTH, 3, 2 * D_FF), 3 ** -0.5),
        'conv_b': nrm(ks[25], (DEPTH, 2 * D_FF), 0.02),
        'w_down': nrm(ks[26], (DEPTH, D_FF, D_MODEL), D_FF ** -0.5),
    }


def reference(x_prompt, x_sample, norm1_g, w_in, shift_mu, decay_w0, decay_up, aaa_a0, aaa_up,
              gate_up, vres_v0, vres_down, vres_up, k_k, k_a, r_k, lnx_g, lnx_b, qn_g, kn_g,
              rpb, w_out, norm2_g, w_up, conv_w, conv_b, w_down):
    prm = {
        'norm1_g': norm1_g, 'w_in': w_in, 'shift_mu': shift_mu,
        'decay_w0': decay_w0, 'decay_up': decay_up, 'aaa_a0': aaa_a0, 'aaa_up': aaa_up,
        'gate_up': gate_up, 'vres_v0': vres_v0, 'vres_down': vres_down, 'vres_up': vres_up,
        'k_k': k_k, 'k_a': k_a, 'r_k': r_k, 'lnx_g': lnx_g, 'lnx_b': lnx_b,
        'qn_g': qn_g, 'kn_g': kn_g, 'rpb': rpb, 'w_out': w_out,
        'norm2_g': norm2_g, 'w_up': w_up, 'conv_w': conv_w, 'conv_b': conv_b, 'w_down': w_down,
    }
    y_prompt = encoder_trunk(x_prompt, prm)
    y_sample = encoder_trunk(x_sample, prm)
    return (y_prompt, y_sample)
```

```python
import numpy as np
from contextlib import ExitStack
import concourse.bass as bass
import concourse.mybir as mybir
from concourse.bass_utils import run_bass_kernel_spmd

F32 = mybir.dt.float32
BF16 = mybir.dt.bfloat16
AF = mybir.ActivationFunctionType
ALU = mybir.AluOpType
AX = mybir.AxisListType

D = 1024
RW = 1920
INC = 3456
DFF = 2816
NL = 2
EXPM05 = float(np.exp(-0.5))


class Buf:
    __slots__ = ("w", "r", "name")

    def __init__(self, name=""):
        self.w = {}
        self.r = {}
        self.name = name
        ALL_BUFS.append(self)


ALL_BUFS = []


class Sched:
    QUEUES = ("pe", "act", "dve", "pool", "sp")
    NLANES = 8

    def __init__(self, nc, es):
        self.nc = nc
        self.eng = {"pe": nc.tensor, "act": nc.scalar, "dve": nc.vector, "pool": nc.gpsimd, "sp": nc.sync}
        self.ops = []
        self.lane_last = {}
        self.lane_rr = {"sp": 0, "pool": 0}
        self.last_on_key = {}
        self.pending = {q: set() for q in self.QUEUES}
        self.sems = {}
        self.emitted = 0
        self.cnt = {}
        self.sigval = {}
        self.waited = {q: {} for q in self.QUEUES}
        self.nw = 0
        for q in self.QUEUES:
            self.sems[q] = es.enter_context(nc.semaphore("s_" + q))
        for q in ("sp", "pool"):
            for l in range(self.NLANES):
                self.sems[(q, l)] = es.enter_context(nc.semaphore("d_%s%d" % (q, l)))

    def _add(self, queue, semkey, fn, reads, writes):
        idx = len(self.ops)
        deps = set()
        for b in reads:
            deps.update(b.w.values())
        for b in writes:
            for k, v in b.w.items():
                if k != semkey:
                    deps.add(v)
            for k, v in b.r.items():
                if k != semkey:
                    deps.add(v)
        deps.update(self.pending[queue])
        self.pending[queue] = set()
        self.ops.append([queue, fn, deps, semkey])
        for b in reads:
            b.r[semkey] = idx
        for b in writes:
            b.w[semkey] = idx
            b.r = {}
        self.last_on_key[semkey] = idx
        return idx

    def op(self, queue, fn, reads=(), writes=()):
        return self._add(queue, queue, fn, reads, writes)

    def dma(self, queue, fn, reads=(), writes=()):
        l = self.lane_rr[queue]
        self.lane_rr[queue] = (l + 1) % self.NLANES
        key = (queue, l)
        idx = self._add(queue, key, fn, reads, writes)
        if key in self.lane_last:
            self.ops[idx][2].add(self.lane_last[key])
        self.lane_last[key] = idx
        return idx

    def barrier(self):
        allp = set(self.last_on_key.values())
        for q in self.QUEUES:
            self.pending[q] = set(allp)
        for b in ALL_BUFS:
            b.w = {}
            b.r = {}

    def flush(self):
        self.barrier()
        ops = self.ops
        start = self.emitted
        signaling = set(self.last_on_key.values())
        for o in ops[start:]:
            signaling.update(o[2])
        for i in range(start, len(ops)):
            k = ops[i][3]
            isdma = isinstance(k, tuple)
            if isdma or i in signaling:
                self.cnt[k] = self.cnt.get(k, 0) + (16 if isdma else 1)
                self.sigval[i] = self.cnt[k]
        waited = self.waited
        for i in range(start, len(ops)):
            queue, fn, deps, k = ops[i]
            need = {}
            for d in deps:
                dk = ops[d][3]
                v = self.sigval[d]
                if need.get(dk, 0) < v:
                    need[dk] = v
            e = self.eng[queue]
            for dk, v in need.items():
                if waited[queue].get(dk, 0) < v:
                    e.wait_ge(self.sems[dk], v)
                    waited[queue][dk] = v
                    self.nw += 1
            if fn is None:
                continue
            inst = fn(e)
            if i in self.sigval:
                inst.then_inc(self.sems[k], 16 if isinstance(k, tuple) else 1)
            ops[i][1] = None
        self.emitted = len(ops)

    def finish(self):
        self.flush()
        for q in self.QUEUES:
            self._add(q, q, None, (), ())
        self.flush()
        return len(self.ops), self.nw


class Ring:
    def __init__(self, items):
        self.items = items
        self.i = 0

    def next(self):
        it = self.items[self.i]
        self.i = (self.i + 1) % len(self.items)
        return it


class Builder:
    def __init__(self, L, NS, depth=NL, debug=False):
        self.L = L
        self.NS = NS
        self.T = L * NS
        self.depth = depth
        self.debug = debug
        self.nc = bass.Bass("TRN2", target_bir_lowering=False)
        self.es = ExitStack()
        self.S = Sched(self.nc, self.es)
        self.consts_np = {}

    def din(self, name, shape, dt=F32):
        return self.nc.dram_tensor(name, list(shape), dt, kind="ExternalInput").ap()

    def dscr(self, name, shape, dt, out=False):
        kind = "ExternalOutput" if (out or self.debug) else "Internal"
        return self.nc.dram_tensor(name, list(shape), dt, kind=kind).ap(), Buf(name)

    def uid(self, name):
        self._uid = getattr(self, "_uid", 0) + 1
        return "%s_%d" % (name, self._uid)

    def sb(self, es, name, shape, dt):
        return es.enter_context(self.nc.sbuf_tensor(self.uid(name), list(shape), dt)), Buf(name)

    def sbring(self, es, name, shape, dt, n):
        return Ring([self.sb(es, "%s%d" % (name, i), shape, dt) for i in range(n)])

    def psring(self, es, name, n, dt=F32, cols=512):
        return Ring([(es.enter_context(self.nc.psum_tensor(self.uid(name), [128, cols], dt)), Buf(name)) for i in range(n)])

    def load(self, out_ap, in_ap, reads, writes, q="sp"):
        self.S.dma(q, lambda e: e.dma_start(out=out_ap, in_=in_ap), reads, writes)

    def store(self, out_ap, in_ap, reads, writes, q="pool"):
        self.S.dma(q, lambda e: e.dma_start(out=out_ap, in_=in_ap), reads, writes)

    def mm(self, out, lhsT, rhs, start, stop, reads, writes):
        self.S.op("pe", lambda e: e.matmul(out, lhsT=lhsT, rhs=rhs, start=start, stop=stop), reads, writes)

    def tr(self, out, in_, ident, reads, writes):
        self.S.op("pe", lambda e: e.transpose(out, in_, ident), reads, writes)

    def act(self, out, in_, func, reads, writes, bias=None, scale=None, accum_out=None):
        kw = {}
        if bias is not None:
            kw["bias"] = bias
        if scale is not None:
            kw["scale"] = scale
        if accum_out is not None:
            kw["accum_out"] = accum_out
        self.S.op("act", lambda e: e.activation(out=out, in_=in_, func=func, **kw), reads, writes)

    def ts(self, q, out, in0, s1, s2, op0, op1, reads, writes):
        if s2 is None:
            self.S.op(q, lambda e: e.tensor_scalar(out=out, in0=in0, scalar1=s1, scalar2=None, op0=op0), reads, writes)
        else:
            self.S.op(q, lambda e: e.tensor_scalar(out=out, in0=in0, scalar1=s1, scalar2=s2, op0=op0, op1=op1), reads, writes)

    def tt(self, q, out, in0, in1, op, reads, writes):
        self.S.op(q, lambda e: e.tensor_tensor(out=out, in0=in0, in1=in1, op=op), reads, writes)

    def stt(self, q, out, in0, scalar, in1, op0, op1, reads, writes):
        self.S.op(q, lambda e: e.scalar_tensor_tensor(out=out, in0=in0, scalar=scalar, in1=in1, op0=op0, op1=op1), reads, writes)

    def cp(self, q, out, in_, reads, writes):
        if q == "act":
            self.S.op(q, lambda e: e.copy(out=out, in_=in_), reads, writes)
        else:
            self.S.op(q, lambda e: e.tensor_copy(out=out, in_=in_), reads, writes)

    def dump(self, name, tile_ap, shape, dt, bufs):
        if not self.debug:
            return
        d = self.nc.dram_tensor(self.uid("dbg_" + name), list(shape), dt, kind="ExternalOutput").ap()
        self.S.dma("sp", lambda e: e.dma_start(out=d, in_=tile_ap), bufs, ())

    def memset(self, q, ap, val, writes):
        self.S.op(q, lambda e: e.memset(ap, val), (), writes)


def host_consts():
    c = {}
    c["ident"] = np.eye(128, dtype=np.float32)
    s = np.arange(128)[:, None]
    t = np.arange(128)[None, :]
    tri = np.zeros((128, 4, 128), np.float32)
    tri[:, 0, :] = (s <= t) * -EXPM05
    tri[:, 1, :] = (s < t) * -EXPM05
    tri[:, 2, :] = (s >= t) * -EXPM05
    tri[:, 3, :] = (s > t) * -EXPM05
    c["tri"] = tri.reshape(128, 512)
    m = np.zeros((128, 4, 128), np.float32)
    m[:, 0, :] = (s < t)
    m[:, 1, :] = (s > t)
    m[:, 2, :] = (s <= t)
    m[:, 3, :] = (s >= t)
    c["masks"] = m.reshape(128, 512)
    bd = np.zeros((128, 128), np.float32)
    bd[:64, :64] = 1
    bd[64:, 64:] = 1
    c["bdones"] = bd
    cols = np.arange(64)
    cs = np.clip(cols - 8, 0, 48)
    kc = np.arange(64)[:, None]
    nm = ((kc >= cs[None, :]) & (kc < cs[None, :] + 16)).astype(np.float32)
    c["natmask"] = np.concatenate([nm, nm], 0)
    return c


def host_params(p):
    o = {}
    f = np.float32

    def fm(a, nch):
        a = np.asarray(a, f)
        sh = a.shape[:-1]
        a = a.reshape(sh + (nch, 128))
        return np.ascontiguousarray(np.moveaxis(a, -1, 0))

    o["g1"] = fm(p["norm1_g"], 8).reshape(128, -1)
    o["g2"] = fm(p["norm2_g"], 8).reshape(128, -1)
    o["mu"] = fm(p["shift_mu"], 15).reshape(128, -1)
    o["w0row"] = np.ascontiguousarray(np.broadcast_to(np.asarray(p["decay_w0"], f).reshape(1, -1), (128, NL * 2 * 512)))
    def padlora(a):
        a = np.asarray(a, f)
        out = np.zeros((128, NL, 2, 512), f)
        for d in range(2):
            out[64 * d:64 * d + 64, :, d, :] = a[:, d].transpose(1, 0, 2)
        return out.reshape(128, -1)
    o["dup"] = padlora(p["decay_up"])
    o["aup"] = padlora(p["aaa_up"])
    o["a0f"] = fm(p["aaa_a0"], 4).reshape(128, -1)
    o["a0row"] = np.ascontiguousarray(np.broadcast_to(np.asarray(p["aaa_a0"], f).reshape(1, -1), (128, NL * 2 * 512)))
    o["gup"] = np.ascontiguousarray(np.asarray(p["gate_up"], f).transpose(1, 0, 2)).reshape(128, -1)
    o["v0f"] = fm(p["vres_v0"], 4).reshape(128, -1)
    o["vdown"] = np.ascontiguousarray(np.asarray(p["vres_down"], f).reshape(1, 4, 128, 32).transpose(2, 0, 1, 3)).reshape(128, -1)
    o["vup"] = np.asarray(p["vres_up"], f).reshape(32, 512)
    o["kkf"] = fm(p["k_k"], 4).reshape(128, -1)
    o["kaf"] = fm(p["k_a"], 4).reshape(128, -1)
    o["karow"] = np.ascontiguousarray(np.broadcast_to(np.asarray(p["k_a"], f).reshape(1, -1), (128, NL * 512)))
    o["rkrow"] = np.ascontiguousarray(np.broadcast_to(np.asarray(p["r_k"], f).reshape(1, -1), (128, NL * 512)))
    o["lgrow"] = np.ascontiguousarray(np.broadcast_to(np.asarray(p["lnx_g"], f).reshape(1, -1), (128, NL * 512)))
    o["lbrow"] = np.ascontiguousarray(np.broadcast_to(np.asarray(p["lnx_b"], f).reshape(1, -1), (128, NL * 512)))
    qg = np.asarray(p["qn_g"], f)
    kg = np.asarray(p["kn_g"], f)
    o["qkg"] = np.ascontiguousarray(np.stack([np.tile(qg, (1, 2)), np.tile(kg, (1, 2))], 1).transpose(2, 0, 1)).reshape(128, -1)
    rpb = np.asarray(p["rpb"], f)
    kc = np.arange(64)[:, None]
    cc = np.arange(64)[None, :]
    ci = np.clip(kc - cc + 15, 0, 30)
    g = rpb[:, :, :, ci]
    t0 = g[:, :, 0:14]
    t1 = g[:, :, 1:15]
    tab = np.concatenate([t0, t1], axis=3)
    o["nattab"] = np.ascontiguousarray(tab.transpose(3, 0, 1, 2, 4)).reshape(128, -1)
    o["convw"] = fm(p["conv_w"], 44).reshape(128, -1)
    o["convb"] = fm(p["conv_b"], 44).reshape(128, -1)
    for k in ("w_in", "w_out", "w_up", "w_down"):
        o[k] = np.ascontiguousarray(np.asarray(p[k], f))
    return o


def weight_conv_gen(B, env, li, stf, stb, storeq):
    P = env["P"]
    g1, g2, g1B, g2B = env["g1"], env["g2"], env["g1B"], env["g2B"]
    k = 0
    for (src, dst, dstB, rows, cols, g) in ((P["w_in"], env["Wb_in"], env["Wb_inB"], D, INC, g1), (P["w_out"], env["Wb_out"], env["Wb_outB"], D, D, None),
                                            (P["w_up"], env["Wb_up"], env["Wb_upB"], D, 2 * DFF, g2), (P["w_down"], env["Wb_dn"], env["Wb_dnB"], DFF, D, None)):
        for kc in range(rows // 128):
            tf, tfB = stf.next()
            tb, tbB = stb.next()
            B.load(tf[:, 0:cols], src[li, kc * 128:(kc + 1) * 128, :], (), (tfB,))
            q = "dve" if k % 2 == 0 else "act"
            k += 1
            if g is not None:
                gcol = g[:, li * 8 + kc:li * 8 + kc + 1]
                if q == "dve":
                    B.ts(q, tb[:, 0:cols], tf[:, 0:cols], gcol, None, ALU.mult, None, (tfB, g1B, g2B), (tbB,))
                else:
                    B.act(tb[:, 0:cols], tf[:, 0:cols], AF.Copy, (tfB, g1B, g2B), (tbB,), scale=gcol)
            else:
                B.cp(q, tb[:, 0:cols], tf[:, 0:cols], (tfB,), (tbB,))
            B.store(dst[li, kc * 128:(kc + 1) * 128, :], tb[:, 0:cols], (tbB,), (dstB,), q=storeq)
            yield


def build_program(L, NS, depth=NL, debug=False, phases="ABCDE"):
    B = Builder(L, NS, depth, debug)
    nc, S = B.nc, B.S
    T = L * NS
    NC = L // 128
    NP = L // 512
    ROWS = L // 64
    hc = host_consts()
    xin = B.din("xin", [T, D])
    P = {}
    shapes = dict(g1=[128, NL * 8], g2=[128, NL * 8], mu=[128, NL * 2 * 15], w0row=[128, NL * 2 * 512], dup=[128, NL * 2 * 512],
                  aup=[128, NL * 2 * 512], a0f=[128, NL * 2 * 4], a0row=[128, NL * 2 * 512], gup=[128, NL * 512], v0f=[128, 4],
                  vdown=[128, 4 * 32], vup=[32, 512], kkf=[128, NL * 4], kaf=[128, NL * 4], karow=[128, NL * 512],
                  rkrow=[128, NL * 512], lgrow=[128, NL * 512], lbrow=[128, NL * 512], qkg=[128, NL * 2],
                  nattab=[128, NL * 8 * 14 * 64], convw=[128, NL * 3 * 44], convb=[128, NL * 44],
                  w_in=[NL, D, INC], w_out=[NL, D, D], w_up=[NL, D, 2 * DFF], w_down=[NL, DFF, D],
                  ident=[128, 128], tri=[128, 512], masks=[128, 512], bdones=[128, 128], natmask=[128, 64])
    for k, sh in shapes.items():
        P[k] = B.din(k, sh)
    y, yB = B.dscr("y", [T, D], F32, out=True)
    yT = [Buf("y%d" % i) for i in range(T // 128)]
    xinB = Buf("xin")
    Wb_in, Wb_inB = B.dscr("Wb_in", [NL, D, INC], BF16)
    Wb_out, Wb_outB = B.dscr("Wb_out", [NL, D, D], BF16)
    Wb_up, Wb_upB = B.dscr("Wb_up", [NL, D, 2 * DFF], BF16)
    Wb_dn, Wb_dnB = B.dscr("Wb_dn", [NL, DFF, D], BF16)
    PT, PTB = B.dscr("PT", [23 * 128, T], BF16)
    VT, VTB = B.dscr("VT", [T, 512], BF16)
    SH = [B.dscr("SH%d" % i, [12 * 128, T], BF16) for i in range(NL)]
    LR, LRB = B.dscr("LR", [3 * 128, T], BF16)
    YD = [B.dscr("YD%d" % i, [T, 512], F32) for i in range(2)]
    MX, MXB = B.dscr("MX", [D, T], BF16)
    NK, NKB = B.dscr("NK", [512, T], BF16)
    KBs, KBsB = B.dscr("KBs", [2, 2, 512, T], BF16)
    VK, VKB = B.dscr("VK", [T, 512], BF16)
    SW, SWB = B.dscr("SW", [2, T, 512], F32)

    with ExitStack() as ces:
        def cload(name, shape, dt=F32, src=None):
            t, b = B.sb(ces, "c_" + name, shape, dt)
            B.load(t[:], (src if src is not None else P[name])[:, :], (), (b,))
            return t, b

        def cload_bf(name, shape):
            tf, bf_ = B.sb(ces, "cf_" + name, shape, F32)
            B.load(tf[:], P[name][:, :], (), (bf_,))
            t, b = B.sb(ces, "c_" + name, shape, BF16)
            B.cp("dve", t[:], tf[:], (bf_,), (b,))
            return t, b

        g1, g1B = cload("g1", [128, NL * 8])
        g2, g2B = cload("g2", [128, NL * 8])
        mu, muB = cload("mu", [128, NL * 30])
        cmid, cmidB = B.sb(ces, "cmid", [128, NL * 15], F32)
        for li in range(NL):
            B.tt("dve", cmid[:, li * 15:(li + 1) * 15], mu[:, li * 30:li * 30 + 15], mu[:, li * 30 + 15:li * 30 + 30], ALU.add, (muB,), (cmidB,))
        B.ts("dve", cmid[:], cmid[:], -1.0, 1.0, ALU.mult, ALU.add, (cmidB,), (cmidB,))
        identb, identB = cload_bf("ident", [128, 128])
        identf, identfB = cload("ident", [128, 128])
        CONST = (identB, identfB, g1B, g2B, muB, cmidB)

        if "W" in phases or True:
            with ExitStack() as es:
                stf = B.sbring(es, "stf", [128, 2 * DFF], F32, 2)
                stb = B.sbring(es, "stb", [128, 2 * DFF], BF16, 2)
                for _ in weight_conv_gen(B, locals(), 0, stf, stb, "sp"):
                    pass
                S.flush()

        for li in range(depth):
            xsrc = xin if li == 0 else y
            if "A" in phases:
                with ExitStack() as es:
                    wb, wbB = B.sb(es, "wbin", [128, 8, INC], BF16)
                    B.load(wb[:], Wb_in[li].rearrange("(k p) n -> p k n", p=128), (Wb_inB,), (wbB,))
                    hT, hTB = B.sb(es, "hT", [128, 8, L], BF16)
                    xr = B.sbring(es, "xa", [128, D], F32, 6)
                    hr = B.sbring(es, "ha", [128, D], BF16, 6)
                    junk, junkB = B.sb(es, "junk", [128, D], BF16)
                    st = B.sbring(es, "stat", [128, 4], F32, 6)
                    og = B.sbring(es, "og", [128, 512], BF16, 6)
                    pst = B.psring(es, "pst", 2, BF16, 1024)
                    psm = B.psring(es, "psm", 6, F32, 512)
                    hTBs = [Buf("hTp%d" % j) for j in range(NP)]
                    pieces = [(s, j) for s in range(NS) for j in range(NP)]

                    def stats(s, j):
                        fins = []
                        for i in range(4 * j, 4 * j + 4):
                            t0 = s * L + i * 128
                            xt, xtB = xr.next()
                            B.load(xt[:], xsrc[t0:t0 + 128, :], (yT[t0 // 128],), (xtB,))
                            fins.append(rms_to_hT(B, xt, xtB, st, hr, pst, junk, junkB, identb, identB, hT[:, :, i * 128:(i + 1) * 128], hTBs[j]))
                        return fins

                    kk_ = [0]

                    def proj(s, j):
                        c0 = s * L + j * 512
                        for oc in range(23):
                            ps, psB = psm.next()
                            for kc in range(8):
                                B.mm(ps[:], wb[:, kc, oc * 128:(oc + 1) * 128], hT[:, kc, j * 512:(j + 1) * 512], kc == 0, kc == 7, (wbB, hTBs[j]), (psB,))
                            o, oB = og.next()
                            B.cp("act" if kk_[0] % 2 else "dve", o[:], ps[:], (psB,), (oB,))
                            kk_[0] += 1
                            B.store(PT[oc * 128:(oc + 1) * 128, c0:c0 + 512], o[:], (oB,), (PTB,))
                        for sub in range(4):
                            ps, psB = psm.next()
                            for kc in range(8):
                                B.mm(ps[:], hT[:, kc, j * 512 + sub * 128:j * 512 + (sub + 1) * 128], wb[:, kc, 2944:3456], kc == 0, kc == 7, (wbB, hTBs[j]), (psB,))
                            o, oB = og.next()
                            B.cp("act" if kk_[0] % 2 else "dve", o[:], ps[:], (psB,), (oB,))
                            kk_[0] += 1
                            B.store(VT[c0 + sub * 128:c0 + (sub + 1) * 128, :], o[:], (oB,), (VTB,))

                    fins = stats(*pieces[0])
                    for f in fins:
                        f()
                    for p, (s, j) in enumerate(pieces):
                        nf = stats(*pieces[p + 1]) if p + 1 < len(pieces) else []
                        proj(s, j)
                        for f in nf:
                            f()
                    S.flush()
            if "Z" in phases:
                with ExitStack() as es:
                    z, zB = B.sb(es, "zz", [128, 2048], BF16)
                    B.memset("dve", z[:], 0.0, (zB,))
                    for kc in range(8):
                        for c in range(0, T, 2048):
                            w = min(2048, T - c)
                            B.store(MX[kc * 128:(kc + 1) * 128, c:c + w], z[:, 0:w], (zB,), (MXB,))
                    S.flush()
            if "B" in phases or "b" in phases:
                do_scan = "B" in phases
                phase_B(B, li, locals())
            if "C" in phases:
                phase_C(B, li, locals())
            if "E" in phases:
                phase_DE(B, li, locals())
        nops, nw = S.finish()
    B.es.close()
    return nc, dict(nops=nops, nwaits=nw)


def rms_to_hT(B, xt, xtB, st, hr, pst, junk, junkB, identb, identB, hT_dst, hTB):
    stt_, stB = st.next()
    B.act(junk[:], xt[:], AF.Square, (xtB,), (junkB, stB), accum_out=stt_[:, 0:1])
    B.ts("dve", stt_[:, 1:2], stt_[:, 0:1], 1.0 / D, 1e-6, ALU.mult, ALU.add, (stB,), (stB,))
    B.act(stt_[:, 2:3], stt_[:, 1:2], AF.Sqrt, (stB,), (stB,))
    B.S.op("dve", lambda e, o=stt_[:, 3:4], a=stt_[:, 2:3]: e.reciprocal(out=o, in_=a), (stB,), (stB,))
    ht, htB = hr.next()
    B.ts("dve", ht[:], xt[:], stt_[:, 3:4], None, ALU.mult, None, (xtB, stB), (htB,))

    def finish():
        pt, ptB = pst.next()
        for kc in range(8):
            B.tr(pt[:, kc * 128:(kc + 1) * 128], ht[:, kc * 128:(kc + 1) * 128], identb[:], (htB, identB), (ptB,))
        B.cp("act", hT_dst, pt[:].rearrange("p (k t) -> p k t", k=8), (ptB,), (hTB,))
    return finish


def phase_DE(B, li, env):
    S = B.S
    L, NS, T, NC = B.L, B.NS, B.T, B.L // 128
    P, y, yT, xin = env["P"], env["y"], env["yT"], env["xin"]
    MX, MXB = env["MX"], env["MXB"]
    identb, identB = env["identb"], env["identB"]
    xsrc = xin if li == 0 else y
    for s in range(NS):
        with ExitStack() as eo:
            hT, hTB = B.sb(eo, "hT2", [128, 8, L + 2], BF16)
            B.memset("pool", hT[:, :, 0:1], 0.0, (hTB,))
            B.memset("pool", hT[:, :, L + 1:L + 2], 0.0, (hTB,))
            with ExitStack() as es:
                wo, woB = B.sb(es, "wo", [128, 8, D], BF16)
                B.load(wo[:], env["Wb_out"][li].rearrange("(k p) n -> p k n", p=128), (env["Wb_outB"],), (woB,))
                xr = B.sbring(es, "xd", [128, D], F32, 3)
                xm = B.sbring(es, "xm", [128, D], F32, 3)
                mr = B.sbring(es, "mixt", [128, 8, 128], BF16, 3)
                hr = B.sbring(es, "hd", [128, D], BF16, 3)
                junk, junkB = B.sb(es, "junkd", [128, D], BF16)
                st = B.sbring(es, "statd", [128, 4], F32, 3)
                pst = B.psring(es, "pstd", 2, BF16, 1024)
                psm = B.psring(es, "psmd", 4, F32, 512)
                pend = None
                for i in range(NC):
                    t0 = s * L + i * 128
                    xt, xtB = xr.next()
                    B.load(xt[:], xsrc[t0:t0 + 128, :], (yT[t0 // 128],), (xtB,))
                    mt, mtB = mr.next()
                    B.load(mt[:], MX[:, t0:t0 + 128].rearrange("(k p) t -> p k t", p=128), (MXB,), (mtB,))
                    xo, xoB = xm.next()
                    for half in range(2):
                        ps, psB = psm.next()
                        for kc in range(8):
                            B.mm(ps[:], mt[:, kc, :], wo[:, kc, half * 512:(half + 1) * 512], kc == 0, kc == 7, (mtB, woB), (psB,))
                        B.tt("dve", xo[:, half * 512:(half + 1) * 512], xt[:, half * 512:(half + 1) * 512], ps[:], ALU.add, (xtB, psB), (xoB,))
                    B.store(y[t0:t0 + 128, :], xo[:], (xoB,), (yT[t0 // 128],))
                    fin = rms_to_hT(B, xo, xoB, st, hr, pst, junk, junkB, identb, identB, hT[:, :, 1 + i * 128:1 + (i + 1) * 128], hTB)
                    if pend is not None:
                        pend()
                    pend = fin
                pend()
                S.flush()
            with ExitStack() as es:
                wd, wdB = B.sb(es, "wdn", [128, 22, D], BF16)
                B.load(wd[:], env["Wb_dn"][li].rearrange("(k p) n -> p k n", p=128), (env["Wb_dnB"],), (wdB,))
                cw, cwB = B.sb(es, "cw", [128, 3 * 44], F32)
                B.load(cw[:], P["convw"][:, li * 132:(li + 1) * 132], (), (cwB,))
                cb, cbB = B.sb(es, "cb", [128, 44], F32)
                B.load(cb[:], P["convb"][:, li * 44:(li + 1) * 44], (), (cbB,))
                aTr = B.sbring(es, "aT", [128, 22, 512], BF16, 2)
                wur = B.sbring(es, "wu", [128, 8, 256], BF16, 3)
                ug = B.sbring(es, "ug", [128, 258], BF16, 3)
                uv = B.sbring(es, "uv", [128, 258], BF16, 3)
                dgr = B.sbring(es, "dgr", [128, 6, 128], BF16, 3)
                sg_ = B.sbring(es, "sgl", [128, 256], BF16, 3)
                xr = B.sbring(es, "xe", [128, D], F32, 2)
                xo_ = B.sbring(es, "xoe", [128, D], F32, 2)
                psm = B.psring(es, "psme", 8, F32, 512)
                Wup = env["Wb_up"]
                for tt_ in range(L // 512):
                    aT, aTB = aTr.next()
                    its = [(j, sub) for j in range(22) for sub in range(2)]
                    est = {}

                    def stage_up(k):
                        j, sub = its[k]
                        if sub == 0:
                            wu, wuB = wur.next()
                            B.load(wu[:, :, 0:128], Wup[li, :, j * 128:(j + 1) * 128].rearrange("(k p) n -> p k n", p=128), (env["Wb_upB"],), (wuB,))
                            B.load(wu[:, :, 128:256], Wup[li, :, DFF + j * 128:DFF + (j + 1) * 128].rearrange("(k p) n -> p k n", p=128), (env["Wb_upB"],), (wuB,))
                            dg, dgB = dgr.next()
                            for tap in range(3):
                                B.ts("dve", dg[:, tap, :], identb[:], cw[:, tap * 44 + j:tap * 44 + j + 1], None, ALU.mult, None, (identB, cwB), (dgB,))
                                B.ts("dve", dg[:, 3 + tap, :], identb[:], cw[:, tap * 44 + 22 + j:tap * 44 + 23 + j], None, ALU.mult, None, (identB, cwB), (dgB,))
                            est["w"] = (wu, wuB, dg, dgB)
                        wu, wuB, dg, dgB = est["w"]
                        c0 = tt_ * 512 + sub * 256
                        pg, pgB = psm.next()
                        pv, pvB = psm.next()
                        for kc in range(8):
                            B.mm(pg[:, 0:258], wu[:, kc, 0:128], hT[:, kc, c0:c0 + 258], kc == 0, kc == 7, (wuB, hTB), (pgB,))
                        for kc in range(8):
                            B.mm(pv[:, 0:258], wu[:, kc, 128:256], hT[:, kc, c0:c0 + 258], kc == 0, kc == 7, (wuB, hTB), (pvB,))
                        g, gB = ug.next()
                        v, vB = uv.next()
                        B.cp("act", g[:], pg[:, 0:258], (pgB,), (gB,))
                        B.cp("act", v[:], pv[:, 0:258], (pvB,), (vB,))
                        est[k] = (g, gB, v, vB, dg, dgB)

                    def stage_conv(k):
                        j, sub = its[k]
                        g, gB, v, vB, dg, dgB = est.pop(k)
                        p2, p2B = psm.next()
                        for tap in range(3):
                            B.mm(p2[:, 0:256], dg[:, tap, :], g[:, tap:tap + 256], tap == 0, tap == 2, (dgB, gB), (p2B,))
                        for tap in range(3):
                            B.mm(p2[:, 256:512], dg[:, 3 + tap, :], v[:, tap:tap + 256], tap == 0, tap == 2, (dgB, vB), (p2B,))
                        sl, slB = sg_.next()
                        B.act(sl[:], p2[:, 0:256], AF.Silu, (p2B, cbB), (slB,), bias=cb[:, j:j + 1])
                        B.S.op("act", lambda e_, o=aT[:, j, sub * 256:(sub + 1) * 256], i_=p2[:, 256:512], bb=cb[:, 22 + j:23 + j]:
                               e_.activation(out=o, in_=i_, func=AF.Identity, bias=bb), (p2B, cbB), (aTB,))
                        B.tt("dve", aT[:, j, sub * 256:(sub + 1) * 256], aT[:, j, sub * 256:(sub + 1) * 256], sl[:], ALU.mult, (aTB, slB), (aTB,))

                    for k in range(len(its) + 1):
                        if k < len(its):
                            stage_up(k)
                        if k >= 1:
                            stage_conv(k - 1)
                    for m in range(4):
                        t0 = s * L + tt_ * 512 + m * 128
                        xt, xtB = xr.next()
                        B.load(xt[:], y[t0:t0 + 128, :], (yT[t0 // 128],), (xtB,))
                        xo, xoB = xo_.next()
                        for half in range(2):
                            ps, psB = psm.next()
                            for j in range(22):
                                B.mm(ps[:], aT[:, j, m * 128:(m + 1) * 128], wd[:, j, half * 512:(half + 1) * 512], j == 0, j == 21, (aTB, wdB), (psB,))
                            B.tt("dve", xo[:, half * 512:(half + 1) * 512], xt[:, half * 512:(half + 1) * 512], ps[:], ALU.add, (xtB, psB), (xoB,))
                        B.store(y[t0:t0 + 128, :], xo[:], (xoB,), (yT[t0 // 128],))
                S.flush()


def make_inmaps(inputs, L, NS, ncores, xs_list):
    hp = host_params(inputs)
    hc = host_consts()
    base = {}
    base.update(hp)
    base.update(hc)
    return [dict(base, xin=np.ascontiguousarray(x, dtype=np.float32)) for x in xs_list]


def phase_C(B, li, env):
    S = B.S
    L, NS, T, NC = B.L, B.NS, B.T, B.L // 128
    ROWS = L // 64
    NP = L // 512
    P = env["P"]
    PT, PTB, VT, VTB, MX, MXB = env["PT"], env["PTB"], env["VT"], env["VTB"], env["MX"], env["MXB"]
    identb, identB = env["identb"], env["identB"]
    with ExitStack() as es:
        tabf, tabfB = B.sb(es, "tabf", [128, 8 * 14 * 64], F32)
        B.load(tabf[:], P["nattab"][:, li * 7168:(li + 1) * 7168], (), (tabfB,))
        nmask, nmaskB = B.sb(es, "nmask", [128, 64], F32)
        B.load(nmask[:], P["natmask"][:, :], (), (nmaskB,))
        EB, EBB = B.sb(es, "EB", [128, 8, 14, 64], BF16)
        B.act(tabf[:], tabf[:], AF.Exp, (tabfB,), (tabfB,))
        B.tt("dve", EB[:].rearrange("p h d c -> p (h d) c"), tabf[:].rearrange("p (a c) -> p a c", c=64),
             nmask[:].unsqueeze(1).broadcast_to([128, 112, 64]), ALU.mult, (tabfB, nmaskB), (EBB,))
        bdf, bdfB = B.sb(es, "bdf", [128, 128], F32)
        B.load(bdf[:], P["bdones"][:, :], (), (bdfB,))
        bdb, bdbB = B.sb(es, "bdb", [128, 128], BF16)
        B.cp("dve", bdb[:], bdf[:], (bdfB,), (bdbB,))
        qkg, qkgB = B.sb(es, "qkg", [128, 2], F32)
        B.load(qkg[:], P["qkg"][:, li * 2:li * 2 + 2], (), (qkgB,))
        raw = B.sbring(es, "craw", [128, L], BF16, 2)
        sq, sqB = B.sb(es, "csq", [128, L], BF16)
        rin = B.sbring(es, "crin", [128, 512], F32, 2)
        qn_r = B.sbring(es, "cqn", [128, L], BF16, 2)
        kn_r = B.sbring(es, "ckn", [128, L], BF16, 2)
        vst = B.sbring(es, "cvst", [128, NC, 128], BF16, 2)
        vev_r = B.sbring(es, "cvev", [128, NC, 2, 65], BF16, 2)
        vod_r = B.sbring(es, "cvod", [128, NC, 2, 65], BF16, 2)
        for ring in (vev_r, vod_r):
            for (t, b) in ring.items:
                B.memset("pool", t[:], 1.0, (b,))
        pb_r = B.sbring(es, "cpb", [128, 2, 4, 64], BF16, 4)
        rec_r = B.sbring(es, "crec", [64, 2], F32, 4)
        na_r = B.sbring(es, "cna", [64, 128], BF16, 8)
        nat_r = B.sbring(es, "cnat", [128, 512], BF16, 2)
        ps_s = B.psring(es, "cps", 4, F32, 512)
        ps_o = B.psring(es, "cpo", 2, F32, 512)
        ps_t = B.psring(es, "cpt", 1, BF16, 1024)
        ps_n = B.psring(es, "cpn", 1, F32, 512)
        for s in range(NS):
            for hp in range(4):
                outs = []
                for which in range(2):
                    rw_, rwB = raw.next()
                    row0 = (15 + 4 * which + hp) * 128
                    B.load(rw_[:], PT[row0:row0 + 128, s * L:(s + 1) * L], (PTB,), (rwB,))
                    B.act(sq[:], rw_[:], AF.Square, (rwB,), (sqB,))
                    dst, dstB = (qn_r if which == 0 else kn_r).next()
                    for j in range(NP):
                        ps, psB = ps_n.next()
                        B.mm(ps[:], bdb[:], sq[:, j * 512:(j + 1) * 512], True, True, (bdbB, sqB), (psB,))
                        rr, rrB = rin.next()
                        if which == 0:
                            B.ts("dve", rr[:], ps[:], 1.0, 64e-6, ALU.mult, ALU.add, (psB,), (rrB,))
                        else:
                            B.ts("dve", rr[:], ps[:], 1.0 / 64, 1e-6, ALU.mult, ALU.add, (psB,), (rrB,))
                        B.act(rr[:], rr[:], AF.Sqrt, (rrB,), (rrB,))
                        B.S.op("dve", lambda e, o=rr[:], a=rr[:]: e.reciprocal(out=o, in_=a), (rrB,), (rrB,))
                        B.stt("dve", dst[:, j * 512:(j + 1) * 512], rw_[:, j * 512:(j + 1) * 512], qkg[:, which:which + 1], rr[:], ALU.mult, ALU.mult,
                              (rwB, qkgB, rrB), (dstB,))
                    outs.append((dst, dstB))
                (qn, qnB), (kn, knB) = outs
                vev, vevB = vev_r.next()
                vod, vodB = vod_r.next()
                st_, stB_ = vst.next()
                B.load(st_[:], VT[s * L:(s + 1) * L, hp * 128:(hp + 1) * 128].rearrange("(j p) f -> p j f", p=128), (VTB,), (stB_,))
                B.cp("pool", vev[:, :, :, 0:64], st_[:].rearrange("p j (e d) -> p j e d", e=2), (stB_,), (vevB,))
                st2, stB2 = vst.next()
                B.load(st2[:, 0:NC - 1, :], VT[s * L + 64:(s + 1) * L - 64, hp * 128:(hp + 1) * 128].rearrange("(j p) f -> p j f", p=128), (VTB,), (stB2,))
                B.cp("pool", vod[:, 0:NC - 1, :, 0:64], st2[:, 0:NC - 1, :].rearrange("p j (e d) -> p j e d", e=2), (stB2,), (vodB,))
                NPAIR = ROWS // 2
                stq = {}
                stp = {}
                ptst = {"pt": None}

                def stage_qk(r2):
                    banks = [ps_s.next(), ps_s.next()]
                    pbs = [pb_r.next(), pb_r.next()]
                    info = []
                    for rr in range(2):
                        r = 2 * r2 + rr
                        rs = min(max(r - 4, 0), ROWS - 8)
                        info.append((r, rs, rs - r + 7))
                    for e in range(2):
                        ps, psB = banks[e]
                        psv = ps[:].rearrange("p (a j c) -> p a j c", a=2, j=4)
                        for rr in range(2):
                            r, rs, dr0 = info[rr]
                            for j in range(4):
                                k0 = (rs + 2 * j) * 64
                                B.mm(psv[:, rr, j, :], kn[64 * e:64 * e + 64, k0:k0 + 128], qn[64 * e:64 * e + 64, r * 64:(r + 1) * 64], True, True, (knB, qnB), (psB,))
                        pb, pbB = pbs[e]
                        B.act(pb[:].rearrange("p a j c -> p (a j c)"), ps[:], AF.Exp, (psB,), (pbB,))
                        for rr in range(2):
                            r, rs, dr0 = info[rr]
                            B.tt("dve", pb[:, rr], pb[:, rr], EB[:, 2 * hp + e, dr0:dr0 + 7:2, :], ALU.mult, (pbB, EBB), (pbB,))
                    stq[r2] = (pbs, info)

                def stage_pv(r2):
                    pbs, info = stq.pop(r2)
                    nas = []
                    for rr in range(2):
                        r, rs, dr0 = info[rr]
                        po, poB = ps_o.next()
                        for e in range(2):
                            pb, pbB = pbs[e]
                            for j in range(4):
                                kr = rs + 2 * j
                                vsrc, vB_ = (vev, vevB) if kr % 2 == 0 else (vod, vodB)
                                B.mm(po[0:64, e * 65:(e + 1) * 65], pb[:, rr, j, :], vsrc[:, kr // 2, e, :], j == 0, j == 3, (pbB, vB_), (poB,))
                        pov = po[0:64, 0:130].rearrange("p (e d) -> p e d", e=2)
                        rc, rcB = rec_r.next()
                        B.S.op("dve", lambda e_, o=rc[:], a=pov[:, :, 64]: e_.reciprocal(out=o, in_=a), (poB,), (rcB,))
                        na, naB = na_r.next()
                        B.tt("dve", na[:].rearrange("p (e d) -> p e d", e=2), pov[:, :, 0:64], rc[:].unsqueeze(2).broadcast_to([64, 2, 64]), ALU.mult, (poB, rcB), (naB,))
                        nas.append((r, na, naB))
                    stp[r2] = nas

                def stage_tr(r2):
                    for (r, na, naB) in stp.pop(r2):
                        if r % 8 == 0:
                            ptst["pt"] = ps_t.next()
                        pt, ptB = ptst["pt"]
                        B.tr(pt[:, (r % 8) * 64:(r % 8 + 1) * 64], na[:], identb[0:64, 0:64], (naB, identB), (ptB,))
                        if r % 8 == 7:
                            nt, ntB = nat_r.next()
                            B.cp("act", nt[:], pt[:, 0:512], (ptB,), (ntB,))
                            c0 = s * L + (r - 7) * 64
                            B.store(MX[(4 + hp) * 128:(5 + hp) * 128, c0:c0 + 512], nt[:], (ntB,), (MXB,))

                for i in range(NPAIR + 2):
                    if i < NPAIR:
                        stage_qk(i)
                    if 0 <= i - 1 < NPAIR:
                        stage_pv(i - 1)
                    if 0 <= i - 2 < NPAIR:
                        stage_tr(i - 2)
        S.flush()


def phase_B(B, li, env):
    S = B.S
    L, NS, T, NC = B.L, B.NS, B.T, B.L // 128
    NP = L // 512
    P = env["P"]
    PT, PTB, MX, MXB, LR, LRB = env["PT"], env["PTB"], env["MX"], env["MXB"], env["LR"], env["LRB"]
    SH, YD = env["SH"], env["YD"]
    SHc, SHcB = SH[li]
    identb, identB = env["identb"], env["identB"]
    mu, muB, cmid, cmidB = env["mu"], env["muB"], env["cmid"], env["cmidB"]

    def cl(es, name, cols, lo, dt=F32):
        t, b = B.sb(es, name, [128, cols], F32)
        B.load(t[:], P[name][:, lo:lo + cols], (), (b,))
        if dt == BF16:
            t2, b2 = B.sb(es, name + "b", [128, cols], BF16)
            B.cp("dve", t2[:], t[:], (b,), (b2,))
            return t2, b2
        return t, b

    with ExitStack() as es:
        raw = B.sbring(es, "braw", [128, L + 2], BF16, 3)
        for (t, b) in raw.items:
            B.memset("pool", t[:], 0.0, (b,))
        tmp = B.sbring(es, "btmp", [128, L], BF16, 2)
        outr = B.sbring(es, "bout", [128, L], BF16, 3)
        vall, vallB = B.sb(es, "vall", [128, 4, L], BF16)
        if li > 0:
            vdown, vdownB = cl(es, "vdown", 128, 0, BF16)
            vupf, vupfB = B.sb(es, "vupf", [32, 512], F32)
            B.load(vupf[:], P["vup"][:, :], (), (vupfB,))
            vup, vupB = B.sb(es, "vupb", [32, 512], BF16)
            B.cp("dve", vup[:], vupf[:], (vupfB,), (vupB,))
            v0f, v0fB = cl(es, "v0f", 4, 0)
            tb_r = B.sbring(es, "vtb", [32, 512], BF16, 2)
            mix_r = B.sbring(es, "vmix", [128, 512], BF16, 2)
            vf_r = B.sbring(es, "vvf", [128, 512], BF16, 2)
            dd_r = B.sbring(es, "vdd", [128, 512], BF16, 2)
            psv = B.psring(es, "bpsv", 4, F32, 512)
        for s in range(NS):
            for c in range(15):
                rw_, rwB = raw.next()
                B.load(rw_[:, 1:L + 1], PT[c * 128:(c + 1) * 128, s * L:(s + 1) * L], (PTB,), (rwB,))
                t1, t1B = tmp.next()
                m0 = mu[:, li * 30 + c:li * 30 + c + 1]
                m1 = mu[:, li * 30 + 15 + c:li * 30 + 16 + c]
                cm = cmid[:, li * 15 + c:li * 15 + c + 1]
                B.ts("dve", t1[:], rw_[:, 0:L], m0, None, ALU.mult, None, (rwB, muB), (t1B,))
                B.stt("dve", t1[:], rw_[:, 2:L + 2], m1, t1[:], ALU.mult, ALU.add, (rwB, muB, t1B), (t1B,))
                if 8 <= c < 12:
                    B.stt("dve", vall[:, c - 8, :], rw_[:, 1:L + 1], cm, t1[:], ALU.mult, ALU.add, (rwB, cmidB, t1B), (vallB,))
                    continue
                o, oB = outr.next()
                B.stt("dve", o[:], rw_[:, 1:L + 1], cm, t1[:], ALU.mult, ALU.add, (rwB, cmidB, t1B), (oB,))
                if c < 8:
                    B.store(SHc[c * 128:(c + 1) * 128, s * L:(s + 1) * L], o[:], (oB,), (SHcB,))
                else:
                    if c == 12:
                        B.act(o[:], o[:], AF.Tanh, (oB,), (oB,))
                    elif c == 14:
                        B.act(o[:], o[:], AF.Sigmoid, (oB,), (oB,))
                    B.store(LR[(c - 12) * 128:(c - 11) * 128, s * L:(s + 1) * L], o[:], (oB,), (LRB,))
            if li > 0:
                SH0, SH0B = SH[0]
                for j in range(NP):
                    pt_, ptB_ = psv.next()
                    for hp in range(4):
                        B.mm(pt_[0:32, :], vdown[:, hp * 32:(hp + 1) * 32], vall[:, hp, j * 512:(j + 1) * 512], hp == 0, hp == 3, (vdownB, vallB), (ptB_,))
                    tb, tbB = tb_r.next()
                    B.cp("act", tb[:], pt_[0:32, :], (ptB_,), (tbB,))
                    for hp in range(4):
                        pm, pmB = psv.next()
                        B.mm(pm[:], vup[0:32, hp * 128:(hp + 1) * 128], tb[:], True, True, (vupB, tbB), (pmB,))
                        mx_, mxB_ = mix_r.next()
                        B.act(mx_[:], pm[:], AF.Sigmoid, (pmB, v0fB), (mxB_,), bias=v0f[:, hp:hp + 1])
                        vf, vfB = vf_r.next()
                        B.load(vf[:], SH0[(8 + hp) * 128:(9 + hp) * 128, s * L + j * 512:s * L + (j + 1) * 512], (SH0B,), (vfB,))
                        dd, ddB = dd_r.next()
                        vsl = vall[:, hp, j * 512:(j + 1) * 512]
                        B.tt("dve", dd[:], vf[:], vsl, ALU.subtract, (vfB, vallB), (ddB,))
                        B.tt("dve", dd[:], dd[:], mx_[:], ALU.mult, (ddB, mxB_), (ddB,))
                        B.tt("dve", vsl, vsl, dd[:], ALU.add, (vallB, ddB), (vallB,))
            for hp in range(4):
                B.store(SHc[(8 + hp) * 128:(9 + hp) * 128, s * L:(s + 1) * L], vall[:, hp, :], (vallB,), (SHcB,))
        S.flush()

    NK, NKB = env["NK"], env["NKB"]
    KBs, KBsB = env["KBs"], env["KBsB"]
    VK, VKB = env["VK"], env["VKB"]
    SW, SWB = env["SW"], env["SWB"]
    with ExitStack() as eo:
        dup, dupB = cl(eo, "dup", 1024, li * 1024, BF16)
        aup, aupB = cl(eo, "aup", 1024, li * 1024, BF16)
        a0f, a0fB = cl(eo, "a0f", 8, li * 8)
        kkf, kkfB = cl(eo, "kkf", 4, li * 4)
        kaf, kafB = cl(eo, "kaf", 4, li * 4)
        omka, omkaB = B.sb(eo, "omka", [128, 4], F32)
        B.ts("dve", omka[:], kaf[:], -1.0, 1.0, ALU.mult, ALU.add, (kafB,), (omkaB,))
        w0bc, w0bcB = cl(eo, "w0row", 1024, li * 1024)
        bdb, bdbB = cl(eo, "bdones", 128, 0, BF16)
        ks_r = B.sbring(eo, "ksp", [128, 512], BF16, 3)
        vs_r = B.sbring(eo, "vsp", [128, 512], BF16, 3)
        tw_r = B.sbring(eo, "twp", [128, 512], BF16, 3)
        ad_r = B.sbring(eo, "adp", [128, 512], BF16, 3)
        ap_r = B.sbring(eo, "app", [128, 512], BF16, 3)
        kk_r = B.sbring(eo, "kkp", [128, 512], BF16, 3)
        sq_r = B.sbring(eo, "sqp", [128, 512], BF16, 3)
        nk_r = B.sbring(eo, "nkp", [128, 512], BF16, 3)
        kb_r = B.sbring(eo, "kbp", [128, 2, 2, 512], BF16, 3)
        rin = B.sbring(eo, "rin", [128, 512], F32, 3)
        swp = B.sbring(eo, "swp", [128, 512], F32, 3)
        swo = B.sbring(eo, "swo", [128, 2, 2, 128], F32, 3)
        vto = B.sbring(eo, "vto", [128, 4, 128], BF16, 3)
        ps = B.psring(eo, "b2aps", 6, F32, 512)
        pst = B.psring(eo, "b2apt", 2, BF16, 1024)
        for s in range(NS):
            for hp in range(4):
                for j in range(NP):
                    g0 = s * L + j * 512
                    pc_ = slice(g0, g0 + 512)
                    rows = slice(hp * 128, (hp + 1) * 128)
                    ksp, kspB = ks_r.next()
                    vsp, vspB = vs_r.next()
                    twp, twpB = tw_r.next()
                    adp, adpB = ad_r.next()
                    B.load(ksp[:], SHc[(4 + hp) * 128:(5 + hp) * 128, pc_], (SHcB,), (kspB,))
                    B.load(vsp[:], SHc[(8 + hp) * 128:(9 + hp) * 128, pc_], (SHcB,), (vspB,))
                    B.load(twp[:], LR[0:128, pc_], (LRB,), (twpB,))
                    B.load(adp[:], LR[128:256, pc_], (LRB,), (adpB,))
                    kkp, kkpB = kk_r.next()
                    sqp, sqpB = sq_r.next()
                    B.act(kkp[:], ksp[:], AF.Copy, (kspB, kkfB), (kkpB,), scale=kkf[:, hp:hp + 1])
                    B.act(sqp[:], kkp[:], AF.Square, (kkpB,), (sqpB,))
                    pn, pnB = ps.next()
                    B.mm(pn[:], bdb[:], sqp[:], True, True, (bdbB, sqpB), (pnB,))
                    rr, rrB = rin.next()
                    B.ts("dve", rr[:], pn[:], 1e-12, None, ALU.max, None, (pnB,), (rrB,))
                    B.act(rr[:], rr[:], AF.Sqrt, (rrB,), (rrB,))
                    B.S.op("dve", lambda e_, o=rr[:], a=rr[:]: e_.reciprocal(out=o, in_=a), (rrB,), (rrB,))
                    nkp, nkpB = nk_r.next()
                    B.stt("dve", nkp[:], kkp[:], -1.0, rr[:], ALU.mult, ALU.mult, (kkpB, rrB), (nkpB,))
                    B.store(NK[rows, pc_], nkp[:], (nkpB,), (NKB,))
                    kbp, kbpB = kb_r.next()
                    for d in range(2):
                        pa, paB = ps.next()
                        B.mm(pa[:], aup[:, d * 512 + hp * 128:d * 512 + (hp + 1) * 128], adp[:], True, True, (aupB, adpB), (paB,))
                        ap_, apB = ap_r.next()
                        B.act(ap_[:], pa[:], AF.Sigmoid, (paB, a0fB), (apB,), bias=a0f[:, d * 4 + hp:d * 4 + hp + 1])
                        B.ts("dve", kbp[:, d, 0, :], ap_[:], kaf[:, hp:hp + 1], omka[:, hp:hp + 1], ALU.mult, ALU.add, (apB, kafB, omkaB), (kbpB,))
                        B.tt("dve", kbp[:, d, 0, :], kbp[:, d, 0, :], ksp[:], ALU.mult, (kbpB, kspB), (kbpB,))
                        B.stt("dve", kbp[:, d, 1, :], nkp[:], -1.0, ap_[:], ALU.mult, ALU.mult, (nkpB, apB), (kbpB,))
                    for d in range(2):
                        B.store(KBs[d, :, rows, pc_].rearrange("k p t -> p k t"), kbp[:, d, :, :], (kbpB,), (KBsB,))
                    for c2 in range(2):
                        pw, pwB = ps.next()
                        for cc in range(2):
                            for d in range(2):
                                B.mm(pw[:, (cc * 2 + d) * 128:(cc * 2 + d + 1) * 128], twp[:, (2 * c2 + cc) * 128:(2 * c2 + cc + 1) * 128],
                                     dup[:, d * 512 + hp * 128:d * 512 + (hp + 1) * 128], True, True, (twpB, dupB), (pwB,))
                        sw, swB = swp.next()
                        w0v = w0bc[:].rearrange("p (d f) -> p d f", d=2)[:, :, hp * 128:(hp + 1) * 128]
                        B.tt("dve", sw[:].rearrange("p (c d f) -> p c d f", c=2, d=2), pw[:].rearrange("p (c d f) -> p c d f", c=2, d=2),
                             w0v.unsqueeze(1).broadcast_to([128, 2, 2, 128]), ALU.add, (pwB, w0bcB), (swB,))
                        so, soB = swo.next()
                        B.act(so[:].rearrange("p c d f -> p (c d f)"), sw[:], AF.Sigmoid, (swB,), (soB,))
                        t0 = g0 + c2 * 256
                        for d in range(2):
                            B.store(SW[d, t0:t0 + 256, rows].rearrange("(c p) f -> p c f", p=128), so[:, :, d, :], (soB,), (SWB,))
                    pt_, ptB_ = pst.next()
                    for cc in range(4):
                        B.tr(pt_[:, cc * 128:(cc + 1) * 128], vsp[:, cc * 128:(cc + 1) * 128], identb[:], (vspB, identB), (ptB_,))
                    vo, voB = vto.next()
                    B.cp("act", vo[:].rearrange("p c f -> p (c f)"), pt_[:, 0:512], (ptB_,), (voB,))
                    B.store(VK[pc_, rows].rearrange("(c p) f -> p c f", p=128), vo[:], (voB,), (VKB,))
        S.flush()

    with ExitStack() as eo:
        tri, triB = cl(eo, "tri", 512, 0)
        msk, mskB = cl(eo, "masks", 512, 0)
        NU = 8
        def rg(n, shape, dt, k):
            return B.sbring(eo, n, shape, dt, k)
        i_rs = rg("i_rs", [128, 128], BF16, 2 * NU)
        i_nk = rg("i_nk", [128, 128], BF16, 2 * NU)
        i_kb = rg("i_kb", [128, 2, 128], BF16, 2 * NU)
        i_vt = rg("i_vt", [128, 128], BF16, 2 * NU)
        i_sw = rg("i_sw", [128, 128], F32, 2 * NU)
        E_r = rg("E", [128, 3, 128], F32, 2 * NU)
        ar_r = rg("ar", [128, 2, 128], BF16, 2 * NU)
        kbt_r = rg("kbt", [128, 2, 128], BF16, 2 * NU)
        MA_r = rg("MA", [128, 2, 2, 128], BF16, 2 * NU)
        RBK_r = rg("RBK", [128, 2, 2, 128], BF16, 2 * NU)
        Wf_r = rg("Wf", [128, 2, 128], BF16, 2 * NU)
        ktb_r = rg("ktb", [128, 2, 128], BF16, NU + 2)
        Ap_r = rg("Ap", [128, 2, 2, 128], BF16, NU // 2 + 1)
        R_r = rg("R", [128, 2, 2, 128], BF16, 2 * NU)
        PA_r = rg("PA", [128, 2, 2, 128], BF16, NU)
        X_r = rg("Xsb", [128, 128], BF16, NU)
        U_r = rg("Usb", [128, 128], BF16, NU)
        Y_r = rg("ysb", [128, 128], F32, NU)
        Hg_r = rg("Hg", [128, 64], F32, NU)
        Hst = {(hp, d): B.sb(eo, "Hst", [128, 64], F32) for hp in range(4) for d in range(2)}
        Hbd = {(hp, d): B.sb(eo, "Hbd", [128, 128], BF16) for hp in range(4) for d in range(2)}
        ps = B.psring(eo, "bps", 8, F32, 512)
        MI = {0: (0, 1, 2), 1: (1, 0, 3)}
        evq = ["act", "dve"]

        def mk(i):
            return msk[:, i * 128:(i + 1) * 128]

        def bc2(ap, n=2):
            return ap.unsqueeze(1).broadcast_to([128, n, 128])

        def groups(units, g):
            return [units[k:k + g] for k in range(0, len(units), g)]

        def v4(t_):
            return t_[:].rearrange("p (a t) -> p a t", a=4)

        def prep(units, s, st):
            for un in units:
                hp, d, c = un
                t0 = s * L + c * 128
                rows = slice(hp * 128, (hp + 1) * 128)
                u = {}
                u["rs"] = i_rs.next()
                u["nk"] = i_nk.next()
                u["kb"] = i_kb.next()
                u["vt"] = i_vt.next()
                u["sw"] = i_sw.next()
                B.load(u["rs"][0][:], SHc[rows, t0:t0 + 128], (SHcB,), (u["rs"][1],))
                B.load(u["nk"][0][:], NK[rows, t0:t0 + 128], (NKB,), (u["nk"][1],))
                B.load(u["kb"][0][:], KBs[d, :, rows, t0:t0 + 128].rearrange("k p t -> p k t"), (KBsB,), (u["kb"][1],))
                B.load(u["vt"][0][:], VK[t0:t0 + 128, rows], (VKB,), (u["vt"][1],))
                B.load(u["sw"][0][:], SW[d, t0:t0 + 128, rows], (SWB,), (u["sw"][1],))
                st[un] = u
            for grp in groups(units, 4):
                bk = [ps.next(), ps.next()]
                for k_, un in enumerate(grp):
                    hp, d, c = un
                    u = st[un]
                    pc, pcB = bk[k_ // 2]
                    o = (k_ % 2) * 256
                    sw, swB = u["sw"]
                    B.mm(pc[:, o:o + 256], sw[:], tri[:, (2 * d) * 128:(2 * d + 2) * 128], True, True, (swB, triB), (pcB,))
                for k_, un in enumerate(grp):
                    u = st[un]
                    pc, pcB = bk[k_ // 2]
                    o = (k_ % 2) * 256
                    E, EB_ = E_r.next()
                    B.act(E[:, 0:2, :].rearrange("p a t -> p (a t)"), pc[:, o:o + 256], AF.Exp, (pcB,), (EB_,))
                    B.act(E[:, 2, :], pc[:, o:o + 128], AF.Exp, (pcB,), (EB_,), scale=-1.0)
                    u["E"] = (E, EB_)
            for un in units:
                u = st[un]
                E, EB_ = u["E"]
                rs_, rsB = u["rs"]
                nk, nkB = u["nk"]
                kb, kbB = u["kb"]
                ar, arB = ar_r.next()
                ktb, ktbB = ktb_r.next()
                B.tt("dve", ar[:, 0, :], nk[:], E[:, 1, :], ALU.mult, (nkB, EB_), (arB,))
                B.tt("pool", ar[:, 1, :], rs_[:], E[:, 0, :], ALU.mult, (rsB, EB_), (arB,))
                B.tt("pool" if (un[0] % 2 == 0) else "dve", ktb[:], kb[:], bc2(E[:, 2, :]), ALU.mult, (kbB, EB_), (ktbB,))
                u["ar"], u["ktb"] = (ar, arB), (ktb, ktbB)
            yield
            for pair in groups(units, 2):
                yield
                C = [ps.next(), ps.next()]
                Ap, ApB = Ap_r.next()
                hb = {}
                for k_, un in enumerate(pair):
                    u = st[un]
                    ar, arB = u["ar"]
                    ktb, ktbB = u["ktb"]
                    arf = ar[:].rearrange("p a t -> p (a t)")
                    for e in range(2):
                        hs = slice(64 * e, 64 * e + 64)
                        bk, bkB = ps.next()
                        hb[(un, e)] = (bk, bkB)
                        B.mm(bk[:, 0:256], ktb[hs, 1, :], arf[hs, :], True, True, (ktbB, arB), (bkB,))
                        B.mm(bk[:, 256:512], ktb[hs, 0, :], arf[hs, :], True, True, (ktbB, arB), (bkB,))
                        cb, cbB = C[e]
                        B.mm(cb[:, k_ * 128:(k_ + 1) * 128], ar[hs, 0, :], ktb[hs, 1, :], True, True, (arB, ktbB), (cbB,))
                    cb, cbB = C[k_]
                    for kb_ in range(2):
                        B.mm(cb[:, 256 + kb_ * 128:256 + (kb_ + 1) * 128], ktb[:, kb_, :], identb[:], True, True, (ktbB, identB), (cbB,))
                for k_, un in enumerate(pair):
                    hp, d, c = un
                    u = st[un]
                    mi = MI[d]
                    MA, MAB = MA_r.next()
                    RBK, RBKB = RBK_r.next()
                    for e in range(2):
                        bk, bkB = hb[(un, e)]
                        B.tt("dve", MA[:, e, :, :], v4(bk)[:, 0:4:2, :], bc2(mk(mi[0])), ALU.mult, (bkB, mskB), (MAB,))
                        B.tt("dve", RBK[:, e, :, :], v4(bk)[:, 1:4:2, :], bc2(mk(mi[2])), ALU.mult, (bkB, mskB), (RBKB,))
                    cb, cbB = C[k_]
                    kbt, kbtB = kbt_r.next()
                    B.cp("dve", kbt[:].rearrange("p a b -> p (a b)"), cb[:, 256:512], (cbB,), (kbtB,))
                    u.update(MA=(MA, MAB), RBK=(RBK, RBKB), kbt=(kbt, kbtB), Ap=(Ap, ApB, k_))
                d0 = pair[0][1]
                for e in range(2):
                    cb, cbB = C[e]
                    B.tt("dve", Ap[:, e, :, :], cb[:, 0:256].rearrange("p (a t) -> p a t", a=2), bc2(mk(MI[d0][1])), ALU.mult, (cbB, mskB), (ApB,))
            for un in units:
                u = st[un]
                MA, MAB = u["MA"]
                Ap, ApB, k_ = u["Ap"]
                u["PM"] = lambda e, MA=MA: MA[:, e, 0, :]
                u["PMB"] = MAB
                u["PA"] = lambda e, Ap=Ap, k_=k_: Ap[:, e, k_, :]
                u["PAB"] = ApB
                u["Wl"] = None
            for lvl in range(7):
                yield
                for pair in groups(units, 2):
                    pab, pabB = ps.next()
                    lb = {}
                    for k_, un in enumerate(pair):
                        u = st[un]
                        lbk, lbkB = ps.next()
                        lb[un] = (lbk, lbkB)
                        PMB, PAB = u["PMB"], u["PAB"]
                        for e in range(2):
                            o = e * 256
                            PA, PM = u["PA"](e), u["PM"](e)
                            if lvl == 0:
                                B.mm(lbk[:, o:o + 128], PA, PM, True, True, (PAB, PMB), (lbkB,))
                                B.mm(lbk[:, o + 128:o + 256], PA, identb[:], True, False, (PAB, identB), (lbkB,))
                                B.mm(lbk[:, o + 128:o + 256], identb[:], identb[:], False, True, (identB,), (lbkB,))
                            else:
                                R, RB_ = u["Wl"]
                                if lvl < 6:
                                    B.mm(lbk[:, o:o + 256], PA, R[:, e, :, :].rearrange("p a t -> p (a t)"), True, False, (PAB, RB_), (lbkB,))
                                else:
                                    B.mm(lbk[:, o + 128:o + 256], PA, R[:, e, 1, :], True, False, (PAB, RB_), (lbkB,))
                                B.mm(lbk[:, o + 128:o + 256], identb[:], R[:, e, 1, :], False, True, (identB, RB_), (lbkB,))
                            if lvl < 6:
                                B.mm(pab[:, (k_ * 2 + e) * 128:(k_ * 2 + e + 1) * 128], PM, PA, True, True, (PAB, PMB), (pabB,))
                    PAn, PAnB = PA_r.next()
                    for k_, un in enumerate(pair):
                        u = st[un]
                        lbk, lbkB = lb[un]
                        q = evq[k_ % 2]
                        if lvl < 6:
                            R2, R2B = R_r.next()
                            B.cp(q, R2[:].rearrange("p e a t -> p (e a t)"), lbk[:], (lbkB,), (R2B,))
                            u["Wl"] = (R2, R2B)
                            u["PM"] = lambda e, R2=R2: R2[:, e, 0, :]
                            u["PMB"] = R2B
                            u["PA"] = lambda e, PAn=PAn, k_=k_: PAn[:, k_, e, :]
                            u["PAB"] = PAnB
                        else:
                            Wf, WfB = Wf_r.next()
                            B.cp(q, Wf[:], v4(lbk)[:, 1:4:2, :], (lbkB,), (WfB,))
                            u["W"] = (Wf, WfB)
                    if lvl < 6:
                        B.cp(evq[(lvl + 1) % 2], PAn[:].rearrange("p k e t -> p (k e t)"), pab[:], (pabB,), (PAnB,))

        def chains(units, st, s):
            for grp in groups(units, 4):
                bk = [ps.next(), ps.next(), ps.next(), ps.next()]
                for k_, un in enumerate(grp):
                    hp, d, c = un
                    u = st[un]
                    ar, arB = u["ar"]
                    MA, MAB = u["MA"]
                    vt, vtB = u["vt"]
                    Hb, HbB = Hbd[(hp, d)]
                    bx, bxB = bk[k_]
                    o = 0
                    B.mm(bx[:, o:o + 128], ar[:, 0, :], Hb[:], True, False, (arB, HbB), (bxB,))
                    for e in range(2):
                        B.mm(bx[:, o + e * 64:o + (e + 1) * 64], MA[:, e, 1, :], vt[:, e * 64:(e + 1) * 64], False, True, (MAB, vtB), (bxB,))
                for k_, un in enumerate(grp):
                    u = st[un]
                    bx, bxB = bk[k_]
                    o = 0
                    u["X"] = X_r.next()
                    B.cp(evq[k_ % 2], u["X"][0][:], bx[:, o:o + 128], (bxB,), (u["X"][1],))
            yield
            yield
            for grp in groups(units, 4):
                bk = [ps.next(), ps.next()]
                for k_, un in enumerate(grp):
                    u = st[un]
                    W, WB = u["W"]
                    X, XB = u["X"]
                    bu, buB = bk[k_ // 2]
                    o = (k_ % 2) * 128
                    for e in range(2):
                        B.mm(bu[:, o + e * 64:o + (e + 1) * 64], W[:, e, :], X[:, e * 64:(e + 1) * 64], True, True, (WB, XB), (buB,))
                for k_, un in enumerate(grp):
                    u = st[un]
                    bu, buB = bk[k_ // 2]
                    o = (k_ % 2) * 128
                    u["U"] = U_r.next()
                    B.cp(evq[(k_ // 2 + 1) % 2], u["U"][0][:], bu[:, o:o + 128], (buB,), (u["U"][1],))
            yield
            yield
            for grp in groups(units, 4):
                bk = [ps.next(), ps.next(), ps.next(), ps.next()]
                for k_, un in enumerate(grp):
                    hp, d, c = un
                    u = st[un]
                    ar, arB = u["ar"]
                    RBK, RBKB = u["RBK"]
                    U, UB = u["U"]
                    vt, vtB = u["vt"]
                    kbt, kbtB = u["kbt"]
                    Hb, HbB = Hbd[(hp, d)]
                    bo, boB = bk[k_ // 2]
                    bh, bhB = bk[2 + k_ // 2]
                    o = (k_ % 2) * 128
                    B.mm(bo[:, o:o + 128], ar[:, 1, :], Hb[:], True, False, (arB, HbB), (boB,))
                    for e in range(2):
                        B.mm(bo[:, o + e * 64:o + (e + 1) * 64], RBK[:, e, 0, :], U[:, e * 64:(e + 1) * 64], False, False, (RBKB, UB), (boB,))
                        B.mm(bo[:, o + e * 64:o + (e + 1) * 64], RBK[:, e, 1, :], vt[:, e * 64:(e + 1) * 64], False, True, (RBKB, vtB), (boB,))
                    B.mm(bh[:, o:o + 128], kbt[:, 1, :], U[:], True, False, (kbtB, UB), (bhB,))
                    B.mm(bh[:, o:o + 128], kbt[:, 0, :], vt[:], False, True, (kbtB, vtB), (bhB,))
                for k_, un in enumerate(grp):
                    hp, d, c = un
                    u = st[un]
                    bo, boB = bk[k_ // 2]
                    bh, bhB = bk[2 + k_ // 2]
                    o = (k_ % 2) * 128
                    E, EB_ = u["E"]
                    Hs, HsB = Hst[(hp, d)]
                    Hb, HbB = Hbd[(hp, d)]
                    gcol = 127 if d == 0 else 0
                    gam = E[:, 0, gcol:gcol + 1]
                    Hg, HgB = Hg_r.next()
                    B.ts("dve", Hg[:], Hs[:], gam, None, ALU.mult, None, (HsB, EB_), (HgB,))
                    for e in range(2):
                        hs = slice(64 * e, 64 * e + 64)
                        B.stt("dve", Hs[hs, :], bh[hs, o + e * 64:o + (e + 1) * 64], gam[hs, :], Hg[hs, :], ALU.mult, ALU.add, (bhB, EB_, HgB), (HsB,))
                        B.cp("act", Hb[hs, e * 64:(e + 1) * 64], Hs[hs, :], (HsB,), (HbB,))
                    ysb, ysbB = Y_r.next()
                    B.cp("act", ysb[:], bo[:, o:o + 128], (boB,), (ysbB,))
                    t0 = s * L + c * 128
                    B.store(YD[d][0][t0:t0 + 128, hp * 128:(hp + 1) * 128], ysb[:], (ysbB,), (YD[d][1],))

        def pipeline(unit_fn, s):
            st = {}
            yield from prep(unit_fn(0), s, st)
            for i in range(NC):
                cur = st
                gens = [chains(unit_fn(i), cur, s)]
                if i + 1 < NC:
                    st = {}
                    gens.append(prep(unit_fn(i + 1), s, st))
                while gens:
                    for g in list(gens):
                        try:
                            next(g)
                        except StopIteration:
                            gens.remove(g)
                    yield

        STAGGER = 6
        for s in range(NS if env.get("do_scan", True) else 0):
            for key in Hst:
                B.memset("pool", Hst[key][0][:], 0.0, (Hst[key][1],))
                B.memset("pool", Hbd[key][0][:], 0.0, (Hbd[key][1],))
            pA = pipeline(lambda i: [(hp, 0, i) for hp in range(4)], s)
            pB = pipeline(lambda i: [(hp, 1, NC - 1 - i) for hp in range(4)], s)
            live = [pA]
            tick = 0
            started_b = False
            while live:
                for g in list(live):
                    try:
                        next(g)
                    except StopIteration:
                        live.remove(g)
                tick += 1
                if tick == STAGGER and not started_b:
                    live.append(pB)
                    started_b = True
        S.flush()
    phase_B3(B, li, env)


def phase_B3(B, li, env):
    S = B.S
    L, NS, T, NC = B.L, B.NS, B.T, B.L // 128
    P = env["P"]
    MX, MXB, LR, LRB = env["MX"], env["MXB"], env["LR"], env["LRB"]
    SHc, SHcB = env["SH"][li]
    YD = env["YD"]
    identb, identB = env["identb"], env["identB"]
    with ExitStack() as es:
        def cl(name, cols, lo, dt=F32):
            t, b = B.sb(es, name, [128, cols], F32)
            B.load(t[:], P[name][:, lo:lo + cols], (), (b,))
            if dt == BF16:
                t2, b2 = B.sb(es, name + "b", [128, cols], BF16)
                B.cp("dve", t2[:], t[:], (b,), (b2,))
                return t2, b2
            return t, b
        aup, aupB = cl("aup", 1024, li * 1024, BF16)
        gup, gupB = cl("gup", 512, li * 512, BF16)
        a0bc, a0bcB = cl("a0row", 1024, li * 1024)
        kabc, kabcB = cl("karow", 512, li * 512)
        rkbcf, rkbcfB = cl("rkrow", 512, li * 512)
        lgbc, lgbcB = cl("lgrow", 512, li * 512)
        lbbc, lbbcB = cl("lbrow", 512, li * 512)
        omka2, omka2B = B.sb(es, "omka2", [128, 512], BF16)
        B.ts("dve", omka2[:], kabc[:], -2.0, 2.0, ALU.mult, ALU.add, (kabcB,), (omka2B,))
        kab, kabB = B.sb(es, "kab16", [128, 512], BF16)
        B.cp("dve", kab[:], kabc[:], (kabcB,), (kabB,))
        rkbc, rkbcB = B.sb(es, "rkb16", [128, 512], BF16)
        B.cp("dve", rkbc[:], rkbcf[:], (rkbcfB,), (rkbcB,))
        y_r = B.sbring(es, "py", [128, 2, 512], F32, 3)
        f_r = B.sbring(es, "pf", [128, 12, 128], BF16, 3)
        l_r = B.sbring(es, "pl", [128, 2, 128], BF16, 3)
        tk_r = B.sbring(es, "ptk", [128, 3, 512], BF16, 3)
        a_r = B.sbring(es, "pa", [128, 2, 512], BF16, 3)
        t_r = B.sbring(es, "ptt", [128, 512], BF16, 2)
        c_r = B.sbring(es, "pcn", [128, 512], F32, 2)
        q_r = B.sbring(es, "psq", [128, 512], F32, 2)
        s_r = B.sbring(es, "pst8", [128, 32], F32, 2)
        o_r = B.sbring(es, "pob", [128, 512], BF16, 2)
        m_r = B.sbring(es, "pmx", [128, 4, 128], BF16, 2)
        ps = B.psring(es, "pps", 5, F32, 512)
        pst = B.psring(es, "ppt", 3, BF16, 1024)
        h8 = lambda ap: ap.rearrange("p (h n) -> p h n", h=8)
        tiles = [(s, i) for s in range(NS) for i in range(NC)]
        b3 = {}
        wgen = None
        if li + 1 < B.depth:
            wstf = B.sbring(es, "wstf", [128, 2 * DFF], F32, 2)
            wstb = B.sbring(es, "wstb", [128, 2 * DFF], BF16, 2)
            wgen = weight_conv_gen(B, env, li + 1, wstf, wstb, "pool")

        def stage1(s, i):
            t0 = s * L + i * 128
            yy, yyB = y_r.next()
            for d in range(2):
                B.load(yy[:, d, :], YD[d][0][t0:t0 + 128, :], (YD[d][1],), (yyB,))
            fm, fmB = f_r.next()
            B.load(fm[:], SHc[:, t0:t0 + 128].rearrange("(k p) t -> p k t", p=128), (SHcB,), (fmB,))
            lr, lrB = l_r.next()
            B.load(lr[:], LR[128:384, t0:t0 + 128].rearrange("(k p) t -> p k t", p=128), (LRB,), (lrB,))
            pa_, paB_ = pst.next()
            pb_, pbB_ = pst.next()
            for k in range(8):
                B.tr(pa_[:, k * 128:(k + 1) * 128], fm[:, k, :], identb[:], (fmB, identB), (paB_,))
            for k in range(4):
                B.tr(pb_[:, k * 128:(k + 1) * 128], fm[:, 8 + k, :], identb[:], (fmB, identB), (pbB_,))
            tk, tkB = tk_r.next()
            B.cp("act", tk[:, 0:2, :].rearrange("p a f -> p (a f)"), pa_[:], (paB_,), (tkB,))
            B.cp("act", tk[:, 2, :], pb_[:, 0:512], (pbB_,), (tkB,))
            aa, aaB = a_r.next()
            for d in range(2):
                pq, pqB = ps.next()
                B.mm(pq[:], lr[:, 0, :], aup[:, d * 512:(d + 1) * 512], True, True, (lrB, aupB), (pqB,))
                B.tt("dve", aa[:, d, :], pq[:], a0bc[:, d * 512:(d + 1) * 512], ALU.add, (pqB, a0bcB), (aaB,))
            B.act(aa[:].rearrange("p a f -> p (a f)"), aa[:].rearrange("p a f -> p (a f)"), AF.Sigmoid, (aaB,), (aaB,))
            b3[(s, i)] = (t0, yy, yyB, tk, tkB, aa, aaB, lr, lrB)

        def stage2(s, i):
            t0, yy, yyB, tk, tkB, aa, aaB, lr, lrB = b3.pop((s, i))
            t1, t1B = t_r.next()
            B.tt("dve", t1[:], aa[:, 0, :], aa[:, 1, :], ALU.add, (aaB,), (t1B,))
            B.tt("dve", t1[:], t1[:], kab[:], ALU.mult, (t1B, kabB), (t1B,))
            B.tt("dve", t1[:], t1[:], omka2[:], ALU.add, (t1B, omka2B), (t1B,))
            B.tt("dve", t1[:], t1[:], tk[:, 1, :], ALU.mult, (t1B, tkB), (t1B,))
            B.tt("dve", t1[:], t1[:], tk[:, 0, :], ALU.mult, (t1B, tkB), (t1B,))
            B.tt("dve", t1[:], t1[:], rkbc[:], ALU.mult, (t1B, rkbcB), (t1B,))
            st, stB = s_r.next()
            B.S.op("dve", lambda e_, o=st[:, 0:8], a=h8(t1[:]): e_.reduce_sum(out=o, in_=a, axis=AX.X), (t1B,), (stB,))
            cn, cnB = c_r.next()
            B.tt("dve", cn[:], yy[:, 0, :], yy[:, 1, :], ALU.add, (yyB,), (cnB,))
            B.S.op("dve", lambda e_, o=st[:, 8:16], a=h8(cn[:]): e_.reduce_sum(out=o, in_=a, axis=AX.X), (cnB,), (stB,))
            B.ts("dve", st[:, 8:16], st[:, 8:16], -1.0 / 64, None, ALU.mult, None, (stB,), (stB,))
            B.tt("dve", h8(cn[:]), h8(cn[:]), st[:, 8:16].unsqueeze(2).broadcast_to([128, 8, 64]), ALU.add, (cnB, stB), (cnB,))
            sqv, sqvB = q_r.next()
            B.tt("dve", sqv[:], cn[:], cn[:], ALU.mult, (cnB,), (sqvB,))
            B.S.op("dve", lambda e_, o=st[:, 16:24], a=h8(sqv[:]): e_.reduce_sum(out=o, in_=a, axis=AX.X), (sqvB,), (stB,))
            B.ts("dve", st[:, 16:24], st[:, 16:24], 1.0 / 64, 64e-5, ALU.mult, ALU.add, (stB,), (stB,))
            B.act(st[:, 16:24], st[:, 16:24], AF.Sqrt, (stB,), (stB,))
            B.S.op("dve", lambda e_, o=st[:, 24:32], a=st[:, 16:24]: e_.reciprocal(out=o, in_=a), (stB,), (stB,))
            B.tt("dve", h8(cn[:]), h8(cn[:]), st[:, 24:32].unsqueeze(2).broadcast_to([128, 8, 64]), ALU.mult, (cnB, stB), (cnB,))
            B.tt("dve", cn[:], cn[:], lgbc[:], ALU.mult, (cnB, lgbcB), (cnB,))
            B.tt("dve", cn[:], cn[:], lbbc[:], ALU.add, (cnB, lbbcB), (cnB,))
            B.tt("dve", h8(sqv[:]), h8(tk[:, 2, :]), st[:, 0:8].unsqueeze(2).broadcast_to([128, 8, 64]), ALU.mult, (tkB, stB, sqvB), (sqvB,))
            B.tt("dve", cn[:], cn[:], sqv[:], ALU.add, (cnB, sqvB), (cnB,))
            pg, pgB = ps.next()
            B.mm(pg[:], lr[:, 1, :], gup[:], True, True, (lrB, gupB), (pgB,))
            ob, obB = o_r.next()
            B.tt("dve", ob[:], cn[:], pg[:], ALU.mult, (cnB, pgB), (obB,))
            po_, poB_ = pst.next()
            for k in range(4):
                B.tr(po_[:, k * 128:(k + 1) * 128], ob[:, k * 128:(k + 1) * 128], identb[:], (obB, identB), (poB_,))
            mx, mxB = m_r.next()
            B.cp("act", mx[:].rearrange("p k t -> p (k t)"), po_[:, 0:512], (poB_,), (mxB,))
            B.store(MX[0:512, t0:t0 + 128].rearrange("(k p) t -> p k t", p=128), mx[:], (mxB,), (MXB,))

        stage1(*tiles[0])
        for p, (s, i) in enumerate(tiles):
            if p + 1 < len(tiles):
                stage1(*tiles[p + 1])
            if wgen is not None:
                next(wgen, None)
            stage2(s, i)
        if wgen is not None:
            for _ in wgen:
                pass
        S.flush()


_CACHE = {}


def kernel(**inputs):
    L, NS, NCORES = 4096, 2, 8
    xp = np.asarray(inputs["x_prompt"], np.float32)
    xs = np.asarray(inputs["x_sample"], np.float32)
    seqs = [xp[i] for i in range(xp.shape[0])] + [xs[i] for i in range(xs.shape[0])]
    nseq = len(seqs)
    zero = np.zeros((L, D), np.float32)
    slots = seqs + [zero] * (NCORES * NS - nseq)
    xs_list = [np.concatenate(slots[c * NS:(c + 1) * NS], axis=0) for c in range(NCORES)]
    if "nc" not in _CACHE:
        _CACHE["nc"] = build_program(L, NS)[0]
    nc = _CACHE["nc"]
    in_maps = make_inmaps(inputs, L, NS, NCORES, xs_list)
    res = run_bass_kernel_spmd(nc, in_maps, core_ids=list(range(NCORES)))
    outs = []
    for c in range(NCORES):
        yc = np.asarray(res.results[c]["y"], np.float32).reshape(NS, L, D)
        for s in range(NS):
            outs.append(yc[s])
    y_prompt = np.stack(outs[:xp.shape[0]], axis=0)
    y_sample = np.stack(outs[xp.shape[0]:nseq], axis=0)
    return (y_prompt, y_sample)
```

```python
import numpy as np
from contextlib import ExitStack
import concourse.bass as bass
import concourse.mybir as mybir
from concourse.bass_utils import run_bass_kernel_spmd

F32 = mybir.dt.float32
BF16 = mybir.dt.bfloat16
AF = mybir.ActivationFunctionType
ALU = mybir.AluOpType
AX = mybir.AxisListType

D = 1024
RW = 1920
INC = 3456
DFF = 2816
NL = 2
EXPM05 = float(np.exp(-0.5))


class Buf:
    __slots__ = ("w", "r", "name")

    def __init__(self, name=""):
        self.w = {}
        self.r = {}
        self.name = name
        ALL_BUFS.append(self)


ALL_BUFS = []


class Sched:
    QUEUES = ("pe", "act", "dve", "pool", "sp")
    NLANES = 8

    def __init__(self, nc, es):
        self.nc = nc
        self.eng = {"pe": nc.tensor, "act": nc.scalar, "dve": nc.vector, "pool": nc.gpsimd, "sp": nc.sync}
        self.ops = []
        self.lane_last = {}
        self.lane_rr = {"sp": 0, "pool": 0}
        self.last_on_key = {}
        self.pending = {q: set() for q in self.QUEUES}
        self.sems = {}
        self.emitted = 0
        self.cnt = {}
        self.sigval = {}
        self.waited = {q: {} for q in self.QUEUES}
        self.nw = 0
        for q in self.QUEUES:
            self.sems[q] = es.enter_context(nc.semaphore("s_" + q))
        for q in ("sp", "pool"):
            for l in range(self.NLANES):
                self.sems[(q, l)] = es.enter_context(nc.semaphore("d_%s%d" % (q, l)))

    def _add(self, queue, semkey, fn, reads, writes):
        idx = len(self.ops)
        deps = set()
        for b in reads:
            deps.update(b.w.values())
        for b in writes:
            for k, v in b.w.items():
                if k != semkey:
                    deps.add(v)
            for k, v in b.r.items():
                if k != semkey:
                    deps.add(v)
        deps.update(self.pending[queue])
        self.pending[queue] = set()
        self.ops.append([queue, fn, deps, semkey])
        for b in reads:
            b.r[semkey] = idx
        for b in writes:
            b.w[semkey] = idx
            b.r = {}
        self.last_on_key[semkey] = idx
        return idx

    def op(self, queue, fn, reads=(), writes=()):
        return self._add(queue, queue, fn, reads, writes)

    def dma(self, queue, fn, reads=(), writes=()):
        l = self.lane_rr[queue]
        self.lane_rr[queue] = (l + 1) % self.NLANES
        key = (queue, l)
        idx = self._add(queue, key, fn, reads, writes)
        if key in self.lane_last:
            self.ops[idx][2].add(self.lane_last[key])
        self.lane_last[key] = idx
        return idx

    def barrier(self):
        allp = set(self.last_on_key.values())
        for q in self.QUEUES:
            self.pending[q] = set(allp)
        for b in ALL_BUFS:
            b.w = {}
            b.r = {}

    def flush(self):
        self.barrier()
        ops = self.ops
        start = self.emitted
        signaling = set(self.last_on_key.values())
        for o in ops[start:]:
            signaling.update(o[2])
        for i in range(start, len(ops)):
            k = ops[i][3]
            isdma = isinstance(k, tuple)
            if isdma or i in signaling:
                self.cnt[k] = self.cnt.get(k, 0) + (16 if isdma else 1)
                self.sigval[i] = self.cnt[k]
        waited = self.waited
        for i in range(start, len(ops)):
            queue, fn, deps, k = ops[i]
            need = {}
            for d in deps:
                dk = ops[d][3]
                v = self.sigval[d]
                if need.get(dk, 0) < v:
                    need[dk] = v
            e = self.eng[queue]
            for dk, v in need.items():
                if waited[queue].get(dk, 0) < v:
                    e.wait_ge(self.sems[dk], v)
                    waited[queue][dk] = v
                    self.nw += 1
            if fn is None:
                continue
            inst = fn(e)
            if i in self.sigval:
                inst.then_inc(self.sems[k], 16 if isinstance(k, tuple) else 1)
            ops[i][1] = None
        self.emitted = len(ops)

    def finish(self):
        self.flush()
        for q in self.QUEUES:
            self._add(q, q, None, (), ())
        self.flush()
        return len(self.ops), self.nw


class Ring:
    def __init__(self, items):
        self.items = items
        self.i = 0

    def next(self):
        it = self.items[self.i]
        self.i = (self.i + 1) % len(self.items)
        return it


class Builder:
    def __init__(self, L, NS, depth=NL, debug=False):
        self.L = L
        self.NS = NS
        self.T = L * NS
        self.depth = depth
        self.debug = debug
        self.nc = bass.Bass("TRN2", target_bir_lowering=False)
        self.es = ExitStack()
        self.S = Sched(self.nc, self.es)
        self.consts_np = {}

    def din(self, name, shape, dt=F32):
        return self.nc.dram_tensor(name, list(shape), dt, kind="ExternalInput").ap()

    def dscr(self, name, shape, dt, out=False):
        kind = "ExternalOutput" if (out or self.debug) else "Internal"
        return self.nc.dram_tensor(name, list(shape), dt, kind=kind).ap(), Buf(name)

    def uid(self, name):
        self._uid = getattr(self, "_uid", 0) + 1
        return "%s_%d" % (name, self._uid)

    def sb(self, es, name, shape, dt):
        return es.enter_context(self.nc.sbuf_tensor(self.uid(name), list(shape), dt)), Buf(name)

    def sbring(self, es, name, shape, dt, n):
        return Ring([self.sb(es, "%s%d" % (name, i), shape, dt) for i in range(n)])

    def psring(self, es, name, n, dt=F32, cols=512):
        return Ring([(es.enter_context(self.nc.psum_tensor(self.uid(name), [128, cols], dt)), Buf(name)) for i in range(n)])

    def load(self, out_ap, in_ap, reads, writes, q="sp"):
        self.S.dma(q, lambda e: e.dma_start(out=out_ap, in_=in_ap), reads, writes)

    def store(self, out_ap, in_ap, reads, writes, q="pool"):
        self.S.dma(q, lambda e: e.dma_start(out=out_ap, in_=in_ap), reads, writes)

    def mm(self, out, lhsT, rhs, start, stop, reads, writes):
        self.S.op("pe", lambda e: e.matmul(out, lhsT=lhsT, rhs=rhs, start=start, stop=stop), reads, writes)

    def tr(self, out, in_, ident, reads, writes):
        self.S.op("pe", lambda e: e.transpose(out, in_, ident), reads, writes)

    def act(self, out, in_, func, reads, writes, bias=None, scale=None, accum_out=None):
        kw = {}
        if bias is not None:
            kw["bias"] = bias
        if scale is not None:
            kw["scale"] = scale
        if accum_out is not None:
            kw["accum_out"] = accum_out
        self.S.op("act", lambda e: e.activation(out=out, in_=in_, func=func, **kw), reads, writes)

    def ts(self, q, out, in0, s1, s2, op0, op1, reads, writes):
        if s2 is None:
            self.S.op(q, lambda e: e.tensor_scalar(out=out, in0=in0, scalar1=s1, scalar2=None, op0=op0), reads, writes)
        else:
            self.S.op(q, lambda e: e.tensor_scalar(out=out, in0=in0, scalar1=s1, scalar2=s2, op0=op0, op1=op1), reads, writes)

    def tt(self, q, out, in0, in1, op, reads, writes):
        self.S.op(q, lambda e: e.tensor_tensor(out=out, in0=in0, in1=in1, op=op), reads, writes)

    def stt(self, q, out, in0, scalar, in1, op0, op1, reads, writes):
        self.S.op(q, lambda e: e.scalar_tensor_tensor(out=out, in0=in0, scalar=scalar, in1=in1, op0=op0, op1=op1), reads, writes)

    def cp(self, q, out, in_, reads, writes):
        if q == "act":
            self.S.op(q, lambda e: e.copy(out=out, in_=in_), reads, writes)
        else:
            self.S.op(q, lambda e: e.tensor_copy(out=out, in_=in_), reads, writes)

    def dump(self, name, tile_ap, shape, dt, bufs):
        if not self.debug:
            return
        d = self.nc.dram_tensor(self.uid("dbg_" + name), list(shape), dt, kind="ExternalOutput").ap()
        self.S.dma("sp", lambda e: e.dma_start(out=d, in_=tile_ap), bufs, ())

    def memset(self, q, ap, val, writes):
        self.S.op(q, lambda e: e.memset(ap, val), (), writes)


def host_consts():
    c = {}
    c["ident"] = np.eye(128, dtype=np.float32)
    s = np.arange(128)[:, None]
    t = np.arange(128)[None, :]
    tri = np.zeros((128, 4, 128), np.float32)
    tri[:, 0, :] = (s <= t) * -EXPM05
    tri[:, 1, :] = (s < t) * -EXPM05
    tri[:, 2, :] = (s >= t) * -EXPM05
    tri[:, 3, :] = (s > t) * -EXPM05
    c["tri"] = tri.reshape(128, 512)
    m = np.zeros((128, 4, 128), np.float32)
    m[:, 0, :] = (s < t)
    m[:, 1, :] = (s > t)
    m[:, 2, :] = (s <= t)
    m[:, 3, :] = (s >= t)
    c["masks"] = m.reshape(128, 512)
    bd = np.zeros((128, 128), np.float32)
    bd[:64, :64] = 1
    bd[64:, 64:] = 1
    c["bdones"] = bd
    cols = np.arange(64)
    cs = np.clip(cols - 8, 0, 48)
    kc = np.arange(64)[:, None]
    nm = ((kc >= cs[None, :]) & (kc < cs[None, :] + 16)).astype(np.float32)
    c["natmask"] = np.concatenate([nm, nm], 0)
    return c


def host_params(p):
    o = {}
    f = np.float32

    def fm(a, nch):
        a = np.asarray(a, f)
        sh = a.shape[:-1]
        a = a.reshape(sh + (nch, 128))
        return np.ascontiguousarray(np.moveaxis(a, -1, 0))

    o["g1"] = fm(p["norm1_g"], 8).reshape(128, -1)
    o["g2"] = fm(p["norm2_g"], 8).reshape(128, -1)
    o["mu"] = fm(p["shift_mu"], 15).reshape(128, -1)
    o["w0row"] = np.ascontiguousarray(np.broadcast_to(np.asarray(p["decay_w0"], f).reshape(1, -1), (128, NL * 2 * 512)))
    def padlora(a):
        a = np.asarray(a, f)
        out = np.zeros((128, NL, 2, 512), f)
        for d in range(2):
            out[64 * d:64 * d + 64, :, d, :] = a[:, d].transpose(1, 0, 2)
        return out.reshape(128, -1)
    o["dup"] = padlora(p["decay_up"])
    o["aup"] = padlora(p["aaa_up"])
    o["a0f"] = fm(p["aaa_a0"], 4).reshape(128, -1)
    o["a0row"] = np.ascontiguousarray(np.broadcast_to(np.asarray(p["aaa_a0"], f).reshape(1, -1), (128, NL * 2 * 512)))
    o["gup"] = np.ascontiguousarray(np.asarray(p["gate_up"], f).transpose(1, 0, 2)).reshape(128, -1)
    o["v0f"] = fm(p["vres_v0"], 4).reshape(128, -1)
    o["vdown"] = np.ascontiguousarray(np.asarray(p["vres_down"], f).reshape(1, 4, 128, 32).transpose(2, 0, 1, 3)).reshape(128, -1)
    o["vup"] = np.asarray(p["vres_up"], f).reshape(32, 512)
    o["kkf"] = fm(p["k_k"], 4).reshape(128, -1)
    o["kaf"] = fm(p["k_a"], 4).reshape(128, -1)
    o["karow"] = np.ascontiguousarray(np.broadcast_to(np.asarray(p["k_a"], f).reshape(1, -1), (128, NL * 512)))
    o["rkrow"] = np.ascontiguousarray(np.broadcast_to(np.asarray(p["r_k"], f).reshape(1, -1), (128, NL * 512)))
    o["lgrow"] = np.ascontiguousarray(np.broadcast_to(np.asarray(p["lnx_g"], f).reshape(1, -1), (128, NL * 512)))
    o["lbrow"] = np.ascontiguousarray(np.broadcast_to(np.asarray(p["lnx_b"], f).reshape(1, -1), (128, NL * 512)))
    qg = np.asarray(p["qn_g"], f)
    kg = np.asarray(p["kn_g"], f)
    o["qkg"] = np.ascontiguousarray(np.stack([np.tile(qg, (1, 2)), np.tile(kg, (1, 2))], 1).transpose(2, 0, 1)).reshape(128, -1)
    rpb = np.asarray(p["rpb"], f)
    kc = np.arange(64)[:, None]
    cc = np.arange(64)[None, :]
    ci = np.clip(kc - cc + 15, 0, 30)
    g = rpb[:, :, :, ci]
    t0 = g[:, :, 0:14]
    t1 = g[:, :, 1:15]
    tab = np.concatenate([t0, t1], axis=3)
    o["nattab"] = np.ascontiguousarray(tab.transpose(3, 0, 1, 2, 4)).reshape(128, -1)
    o["convw"] = fm(p["conv_w"], 44).reshape(128, -1)
    o["convb"] = fm(p["conv_b"], 44).reshape(128, -1)
    for k in ("w_in", "w_out", "w_up", "w_down"):
        o[k] = np.ascontiguousarray(np.asarray(p[k], f))
    return o


def weight_conv_gen(B, env, li, stf, stb, storeq, parts=(0, 1, 2, 3)):
    P = env["P"]
    g1, g2, g1B, g2B = env["g1"], env["g2"], env["g1B"], env["g2B"]
    k = 0
    allw = ((P["w_in"], env["Wb_in"], env["Wb_inB"], D, INC, g1), (P["w_out"], env["Wb_out"], env["Wb_outB"], D, D, None),
            (P["w_up"], env["Wb_up"], env["Wb_upB"], D, 2 * DFF, g2), (P["w_down"], env["Wb_dn"], env["Wb_dnB"], DFF, D, None))
    for (src, dst, dstB, rows, cols, g) in [allw[i] for i in parts]:
        for kc in range(rows // 128):
            tf, tfB = stf.next()
            tb, tbB = stb.next()
            B.load(tf[:, 0:cols], src[li, kc * 128:(kc + 1) * 128, :], (), (tfB,))
            q = "dve" if k % 2 == 0 else "act"
            k += 1
            if g is not None:
                gcol = g[:, li * 8 + kc:li * 8 + kc + 1]
                if q == "dve":
                    B.ts(q, tb[:, 0:cols], tf[:, 0:cols], gcol, None, ALU.mult, None, (tfB, g1B, g2B), (tbB,))
                else:
                    B.act(tb[:, 0:cols], tf[:, 0:cols], AF.Copy, (tfB, g1B, g2B), (tbB,), scale=gcol)
            else:
                B.cp(q, tb[:, 0:cols], tf[:, 0:cols], (tfB,), (tbB,))
            B.store(dst[li, kc * 128:(kc + 1) * 128, :], tb[:, 0:cols], (tbB,), (dstB,), q=storeq)
            yield


def build_program(L, NS, depth=NL, debug=False, phases="ABCDE"):
    B = Builder(L, NS, depth, debug)
    nc, S = B.nc, B.S
    T = L * NS
    NC = L // 128
    NP = L // 512
    ROWS = L // 64
    hc = host_consts()
    xin = B.din("xin", [T, D])
    P = {}
    shapes = dict(g1=[128, NL * 8], g2=[128, NL * 8], mu=[128, NL * 2 * 15], w0row=[128, NL * 2 * 512], dup=[128, NL * 2 * 512],
                  aup=[128, NL * 2 * 512], a0f=[128, NL * 2 * 4], a0row=[128, NL * 2 * 512], gup=[128, NL * 512], v0f=[128, 4],
                  vdown=[128, 4 * 32], vup=[32, 512], kkf=[128, NL * 4], kaf=[128, NL * 4], karow=[128, NL * 512],
                  rkrow=[128, NL * 512], lgrow=[128, NL * 512], lbrow=[128, NL * 512], qkg=[128, NL * 2],
                  nattab=[128, NL * 8 * 14 * 64], convw=[128, NL * 3 * 44], convb=[128, NL * 44],
                  w_in=[NL, D, INC], w_out=[NL, D, D], w_up=[NL, D, 2 * DFF], w_down=[NL, DFF, D],
                  ident=[128, 128], tri=[128, 512], masks=[128, 512], bdones=[128, 128], natmask=[128, 64])
    for k, sh in shapes.items():
        P[k] = B.din(k, sh)
    y, yB = B.dscr("y", [T, D], F32, out=True)
    yT = [Buf("y%d" % i) for i in range(T // 128)]
    xinB = Buf("xin")
    Wb_in, Wb_inB = B.dscr("Wb_in", [NL, D, INC], BF16)
    Wb_out, Wb_outB = B.dscr("Wb_out", [NL, D, D], BF16)
    Wb_up, Wb_upB = B.dscr("Wb_up", [NL, D, 2 * DFF], BF16)
    Wb_dn, Wb_dnB = B.dscr("Wb_dn", [NL, DFF, D], BF16)
    PT, PTB = B.dscr("PT", [23 * 128, T], BF16)
    VT, VTB = B.dscr("VT", [T, 512], BF16)
    SH = [B.dscr("SH%d" % i, [12 * 128, T], BF16) for i in range(NL)]
    LR, LRB = B.dscr("LR", [3 * 128, T], BF16)
    YD = [B.dscr("YD%d" % i, [T, 512], F32) for i in range(2)]
    MX, MXB = B.dscr("MX", [D, T], BF16)
    NK, NKB = B.dscr("NK", [512, T], BF16)
    KBs, KBsB = B.dscr("KBs", [2, 2, 512, T], BF16)
    VK, VKB = B.dscr("VK", [T, 512], BF16)
    SW, SWB = B.dscr("SW", [2, T, 512], F32)

    with ExitStack() as ces:
        def cload(name, shape, dt=F32, src=None):
            t, b = B.sb(ces, "c_" + name, shape, dt)
            B.load(t[:], (src if src is not None else P[name])[:, :], (), (b,))
            return t, b

        def cload_bf(name, shape):
            tf, bf_ = B.sb(ces, "cf_" + name, shape, F32)
            B.load(tf[:], P[name][:, :], (), (bf_,))
            t, b = B.sb(ces, "c_" + name, shape, BF16)
            B.cp("dve", t[:], tf[:], (bf_,), (b,))
            return t, b

        g1, g1B = cload("g1", [128, NL * 8])
        g2, g2B = cload("g2", [128, NL * 8])
        mu, muB = cload("mu", [128, NL * 30])
        cmid, cmidB = B.sb(ces, "cmid", [128, NL * 15], F32)
        for li in range(NL):
            B.tt("dve", cmid[:, li * 15:(li + 1) * 15], mu[:, li * 30:li * 30 + 15], mu[:, li * 30 + 15:li * 30 + 30], ALU.add, (muB,), (cmidB,))
        B.ts("dve", cmid[:], cmid[:], -1.0, 1.0, ALU.mult, ALU.add, (cmidB,), (cmidB,))
        identb, identB = cload_bf("ident", [128, 128])
        identf, identfB = cload("ident", [128, 128])
        CONST = (identB, identfB, g1B, g2B, muB, cmidB)

        if "W" in phases or True:
            with ExitStack() as es:
                stf = B.sbring(es, "stf", [128, 2 * DFF], F32, 2)
                stb = B.sbring(es, "stb", [128, 2 * DFF], BF16, 2)
                for _ in weight_conv_gen(B, locals(), 0, stf, stb, "sp", parts=(0,)):
                    pass
                S.flush()

        for li in range(depth):
            xsrc = xin if li == 0 else y
            if "A" in phases:
                with ExitStack() as es:
                    wb, wbB = B.sb(es, "wbin", [128, 8, INC], BF16)
                    B.load(wb[:], Wb_in[li].rearrange("(k p) n -> p k n", p=128), (Wb_inB,), (wbB,))
                    hT, hTB = B.sb(es, "hT", [128, 8, L], BF16)
                    xr = B.sbring(es, "xa", [128, D], F32, 6)
                    hr = B.sbring(es, "ha", [128, D], BF16, 6)
                    junk, junkB = B.sb(es, "junk", [128, D], BF16)
                    st = B.sbring(es, "stat", [128, 4], F32, 6)
                    og = B.sbring(es, "og", [128, 512], BF16, 6)
                    pst = B.psring(es, "pst", 2, BF16, 1024)
                    psm = B.psring(es, "psm", 6, F32, 512)
                    hTBs = [Buf("hTp%d" % j) for j in range(NP)]
                    pieces = [(s, j) for s in range(NS) for j in range(NP)]

                    def stats(s, j):
                        fins = []
                        for i in range(4 * j, 4 * j + 4):
                            t0 = s * L + i * 128
                            xt, xtB = xr.next()
                            B.load(xt[:], xsrc[t0:t0 + 128, :], (yT[t0 // 128],), (xtB,))
                            fins.append(rms_to_hT(B, xt, xtB, st, hr, pst, junk, junkB, identb, identB, hT[:, :, i * 128:(i + 1) * 128], hTBs[j]))
                        return fins

                    kk_ = [0]

                    def proj(s, j):
                        c0 = s * L + j * 512
                        for oc in range(23):
                            ps, psB = psm.next()
                            for kc in range(8):
                                B.mm(ps[:], wb[:, kc, oc * 128:(oc + 1) * 128], hT[:, kc, j * 512:(j + 1) * 512], kc == 0, kc == 7, (wbB, hTBs[j]), (psB,))
                            o, oB = og.next()
                            B.cp("act" if kk_[0] % 2 else "dve", o[:], ps[:], (psB,), (oB,))
                            kk_[0] += 1
                            B.store(PT[oc * 128:(oc + 1) * 128, c0:c0 + 512], o[:], (oB,), (PTB,))
                        for sub in range(4):
                            ps, psB = psm.next()
                            for kc in range(8):
                                B.mm(ps[:], hT[:, kc, j * 512 + sub * 128:j * 512 + (sub + 1) * 128], wb[:, kc, 2944:3456], kc == 0, kc == 7, (wbB, hTBs[j]), (psB,))
                            o, oB = og.next()
                            B.cp("act" if kk_[0] % 2 else "dve", o[:], ps[:], (psB,), (oB,))
                            kk_[0] += 1
                            B.store(VT[c0 + sub * 128:c0 + (sub + 1) * 128, :], o[:], (oB,), (VTB,))

                    fins = stats(*pieces[0])
                    for f in fins:
                        f()
                    for p, (s, j) in enumerate(pieces):
                        nf = stats(*pieces[p + 1]) if p + 1 < len(pieces) else []
                        proj(s, j)
                        for f in nf:
                            f()
                    S.flush()
            if "Z" in phases:
                with ExitStack() as es:
                    z, zB = B.sb(es, "zz", [128, 2048], BF16)
                    B.memset("dve", z[:], 0.0, (zB,))
                    for kc in range(8):
                        for c in range(0, T, 2048):
                            w = min(2048, T - c)
                            B.store(MX[kc * 128:(kc + 1) * 128, c:c + w], z[:, 0:w], (zB,), (MXB,))
                    S.flush()
            if "B" in phases or "b" in phases:
                do_scan = "B" in phases
                phase_B(B, li, locals())
            if "C" in phases:
                phase_C(B, li, locals())
            if "E" in phases:
                phase_DE(B, li, locals())
        nops, nw = S.finish()
    B.es.close()
    return nc, dict(nops=nops, nwaits=nw)


def rms_to_hT(B, xt, xtB, st, hr, pst, junk, junkB, identb, identB, hT_dst, hTB):
    stt_, stB = st.next()
    B.act(junk[:], xt[:], AF.Square, (xtB,), (junkB, stB), accum_out=stt_[:, 0:1])
    B.ts("dve", stt_[:, 1:2], stt_[:, 0:1], 1.0 / D, 1e-6, ALU.mult, ALU.add, (stB,), (stB,))
    B.act(stt_[:, 2:3], stt_[:, 1:2], AF.Sqrt, (stB,), (stB,))
    B.S.op("dve", lambda e, o=stt_[:, 3:4], a=stt_[:, 2:3]: e.reciprocal(out=o, in_=a), (stB,), (stB,))
    ht, htB = hr.next()
    B.ts("dve", ht[:], xt[:], stt_[:, 3:4], None, ALU.mult, None, (xtB, stB), (htB,))

    def finish():
        pt, ptB = pst.next()
        for kc in range(8):
            B.tr(pt[:, kc * 128:(kc + 1) * 128], ht[:, kc * 128:(kc + 1) * 128], identb[:], (htB, identB), (ptB,))
        B.cp("act", hT_dst, pt[:].rearrange("p (k t) -> p k t", k=8), (ptB,), (hTB,))
    return finish


def phase_DE(B, li, env):
    S = B.S
    L, NS, T, NC = B.L, B.NS, B.T, B.L // 128
    P, y, yT, xin = env["P"], env["y"], env["yT"], env["xin"]
    MX, MXB = env["MX"], env["MXB"]
    identb, identB = env["identb"], env["identB"]
    xsrc = xin if li == 0 else y
    for s in range(NS):
        with ExitStack() as eo:
            hT, hTB = B.sb(eo, "hT2", [128, 8, L + 2], BF16)
            B.memset("pool", hT[:, :, 0:1], 0.0, (hTB,))
            B.memset("pool", hT[:, :, L + 1:L + 2], 0.0, (hTB,))
            with ExitStack() as es:
                wo, woB = B.sb(es, "wo", [128, 8, D], BF16)
                B.load(wo[:], env["Wb_out"][li].rearrange("(k p) n -> p k n", p=128), (env["Wb_outB"],), (woB,))
                xr = B.sbring(es, "xd", [128, D], F32, 3)
                xm = B.sbring(es, "xm", [128, D], F32, 3)
                mr = B.sbring(es, "mixt", [128, 8, 128], BF16, 3)
                hr = B.sbring(es, "hd", [128, D], BF16, 3)
                junk, junkB = B.sb(es, "junkd", [128, D], BF16)
                st = B.sbring(es, "statd", [128, 4], F32, 3)
                pst = B.psring(es, "pstd", 2, BF16, 1024)
                psm = B.psring(es, "psmd", 4, F32, 512)
                pend = None
                for i in range(NC):
                    t0 = s * L + i * 128
                    xt, xtB = xr.next()
                    B.load(xt[:], xsrc[t0:t0 + 128, :], (yT[t0 // 128],), (xtB,))
                    mt, mtB = mr.next()
                    B.load(mt[:], MX[:, t0:t0 + 128].rearrange("(k p) t -> p k t", p=128), (MXB,), (mtB,))
                    xo, xoB = xm.next()
                    for half in range(2):
                        ps, psB = psm.next()
                        for kc in range(8):
                            B.mm(ps[:], mt[:, kc, :], wo[:, kc, half * 512:(half + 1) * 512], kc == 0, kc == 7, (mtB, woB), (psB,))
                        B.tt("dve", xo[:, half * 512:(half + 1) * 512], xt[:, half * 512:(half + 1) * 512], ps[:], ALU.add, (xtB, psB), (xoB,))
                    B.store(y[t0:t0 + 128, :], xo[:], (xoB,), (yT[t0 // 128],))
                    fin = rms_to_hT(B, xo, xoB, st, hr, pst, junk, junkB, identb, identB, hT[:, :, 1 + i * 128:1 + (i + 1) * 128], hTB)
                    if pend is not None:
                        pend()
                    pend = fin
                pend()
                S.flush()
            with ExitStack() as es:
                wd, wdB = B.sb(es, "wdn", [128, 22, D], BF16)
                B.load(wd[:], env["Wb_dn"][li].rearrange("(k p) n -> p k n", p=128), (env["Wb_dnB"],), (wdB,))
                cw, cwB = B.sb(es, "cw", [128, 3 * 44], F32)
                B.load(cw[:], P["convw"][:, li * 132:(li + 1) * 132], (), (cwB,))
                cb, cbB = B.sb(es, "cb", [128, 44], F32)
                B.load(cb[:], P["convb"][:, li * 44:(li + 1) * 44], (), (cbB,))
                aTr = B.sbring(es, "aT", [128, 22, 512], BF16, 2)
                wur = B.sbring(es, "wu", [128, 8, 256], BF16, 3)
                ug = B.sbring(es, "ug", [128, 258], BF16, 3)
                uv = B.sbring(es, "uv", [128, 258], BF16, 3)
                dgr = B.sbring(es, "dgr", [128, 6, 128], BF16, 3)
                sg_ = B.sbring(es, "sgl", [128, 256], BF16, 3)
                xr = B.sbring(es, "xe", [128, D], F32, 2)
                xo_ = B.sbring(es, "xoe", [128, D], F32, 2)
                psm = B.psring(es, "psme", 8, F32, 512)
                Wup = env["Wb_up"]
                for tt_ in range(L // 512):
                    aT, aTB = aTr.next()
                    its = [(j, sub) for j in range(22) for sub in range(2)]
                    est = {}

                    def stage_up(k):
                        j, sub = its[k]
                        if sub == 0:
                            wu, wuB = wur.next()
                            B.load(wu[:, :, 0:128], Wup[li, :, j * 128:(j + 1) * 128].rearrange("(k p) n -> p k n", p=128), (env["Wb_upB"],), (wuB,))
                            B.load(wu[:, :, 128:256], Wup[li, :, DFF + j * 128:DFF + (j + 1) * 128].rearrange("(k p) n -> p k n", p=128), (env["Wb_upB"],), (wuB,))
                            dg, dgB = dgr.next()
                            for tap in range(3):
                                B.ts("dve", dg[:, tap, :], identb[:], cw[:, tap * 44 + j:tap * 44 + j + 1], None, ALU.mult, None, (identB, cwB), (dgB,))
                                B.ts("dve", dg[:, 3 + tap, :], identb[:], cw[:, tap * 44 + 22 + j:tap * 44 + 23 + j], None, ALU.mult, None, (identB, cwB), (dgB,))
                            est["w"] = (wu, wuB, dg, dgB)
                        wu, wuB, dg, dgB = est["w"]
                        c0 = tt_ * 512 + sub * 256
                        pg, pgB = psm.next()
                        pv, pvB = psm.next()
                        for kc in range(8):
                            B.mm(pg[:, 0:258], wu[:, kc, 0:128], hT[:, kc, c0:c0 + 258], kc == 0, kc == 7, (wuB, hTB), (pgB,))
                        for kc in range(8):
                            B.mm(pv[:, 0:258], wu[:, kc, 128:256], hT[:, kc, c0:c0 + 258], kc == 0, kc == 7, (wuB, hTB), (pvB,))
                        g, gB = ug.next()
                        v, vB = uv.next()
                        B.cp("act", g[:], pg[:, 0:258], (pgB,), (gB,))
                        B.cp("act", v[:], pv[:, 0:258], (pvB,), (vB,))
                        est[k] = (g, gB, v, vB, dg, dgB)

                    def stage_conv(k):
                        j, sub = its[k]
                        g, gB, v, vB, dg, dgB = est.pop(k)
                        p2, p2B = psm.next()
                        for tap in range(3):
                            B.mm(p2[:, 0:256], dg[:, tap, :], g[:, tap:tap + 256], tap == 0, tap == 2, (dgB, gB), (p2B,))
                        for tap in range(3):
                            B.mm(p2[:, 256:512], dg[:, 3 + tap, :], v[:, tap:tap + 256], tap == 0, tap == 2, (dgB, vB), (p2B,))
                        sl, slB = sg_.next()
                        B.act(sl[:], p2[:, 0:256], AF.Silu, (p2B, cbB), (slB,), bias=cb[:, j:j + 1])
                        B.S.op("act", lambda e_, o=aT[:, j, sub * 256:(sub + 1) * 256], i_=p2[:, 256:512], bb=cb[:, 22 + j:23 + j]:
                               e_.activation(out=o, in_=i_, func=AF.Identity, bias=bb), (p2B, cbB), (aTB,))
                        B.tt("dve", aT[:, j, sub * 256:(sub + 1) * 256], aT[:, j, sub * 256:(sub + 1) * 256], sl[:], ALU.mult, (aTB, slB), (aTB,))

                    for k in range(len(its) + 1):
                        if k < len(its):
                            stage_up(k)
                        if k >= 1:
                            stage_conv(k - 1)
                    for m in range(4):
                        t0 = s * L + tt_ * 512 + m * 128
                        xt, xtB = xr.next()
                        B.load(xt[:], y[t0:t0 + 128, :], (yT[t0 // 128],), (xtB,))
                        xo, xoB = xo_.next()
                        for half in range(2):
                            ps, psB = psm.next()
                            for j in range(22):
                                B.mm(ps[:], aT[:, j, m * 128:(m + 1) * 128], wd[:, j, half * 512:(half + 1) * 512], j == 0, j == 21, (aTB, wdB), (psB,))
                            B.tt("dve", xo[:, half * 512:(half + 1) * 512], xt[:, half * 512:(half + 1) * 512], ps[:], ALU.add, (xtB, psB), (xoB,))
                        B.store(y[t0:t0 + 128, :], xo[:], (xoB,), (yT[t0 // 128],))
                S.flush()


def make_inmaps(inputs, L, NS, ncores, xs_list):
    hp = host_params(inputs)
    hc = host_consts()
    base = {}
    base.update(hp)
    base.update(hc)
    return [dict(base, xin=np.ascontiguousarray(x, dtype=np.float32)) for x in xs_list]


def phase_C(B, li, env):
    S = B.S
    L, NS, T, NC = B.L, B.NS, B.T, B.L // 128
    ROWS = L // 64
    NP = L // 512
    P = env["P"]
    PT, PTB, VT, VTB, MX, MXB = env["PT"], env["PTB"], env["VT"], env["VTB"], env["MX"], env["MXB"]
    identb, identB = env["identb"], env["identB"]
    with ExitStack() as es:
        tabf, tabfB = B.sb(es, "tabf", [128, 8 * 14 * 64], F32)
        B.load(tabf[:], P["nattab"][:, li * 7168:(li + 1) * 7168], (), (tabfB,))
        nmask, nmaskB = B.sb(es, "nmask", [128, 64], F32)
        B.load(nmask[:], P["natmask"][:, :], (), (nmaskB,))
        EB, EBB = B.sb(es, "EB", [128, 8, 14, 64], BF16)
        B.act(tabf[:], tabf[:], AF.Exp, (tabfB,), (tabfB,))
        B.tt("dve", EB[:].rearrange("p h d c -> p (h d) c"), tabf[:].rearrange("p (a c) -> p a c", c=64),
             nmask[:].unsqueeze(1).broadcast_to([128, 112, 64]), ALU.mult, (tabfB, nmaskB), (EBB,))
        bdf, bdfB = B.sb(es, "bdf", [128, 128], F32)
        B.load(bdf[:], P["bdones"][:, :], (), (bdfB,))
        bdb, bdbB = B.sb(es, "bdb", [128, 128], BF16)
        B.cp("dve", bdb[:], bdf[:], (bdfB,), (bdbB,))
        qkg, qkgB = B.sb(es, "qkg", [128, 2], F32)
        B.load(qkg[:], P["qkg"][:, li * 2:li * 2 + 2], (), (qkgB,))
        raw = B.sbring(es, "craw", [128, L], BF16, 2)
        sq, sqB = B.sb(es, "csq", [128, L], BF16)
        rin = B.sbring(es, "crin", [128, 512], F32, 2)
        qn_r = B.sbring(es, "cqn", [128, L], BF16, 2)
        kn_r = B.sbring(es, "ckn", [128, L], BF16, 2)
        vst = B.sbring(es, "cvst", [128, NC, 128], BF16, 2)
        vev_r = B.sbring(es, "cvev", [128, NC, 2, 65], BF16, 2)
        vod_r = B.sbring(es, "cvod", [128, NC, 2, 65], BF16, 2)
        for ring in (vev_r, vod_r):
            for (t, b) in ring.items:
                B.memset("pool", t[:], 1.0, (b,))
        pb_r = B.sbring(es, "cpb", [128, 2, 4, 64], BF16, 4)
        rec_r = B.sbring(es, "crec", [64, 2], F32, 4)
        na_r = B.sbring(es, "cna", [64, 128], BF16, 8)
        nat_r = B.sbring(es, "cnat", [128, 512], BF16, 2)
        ps_s = B.psring(es, "cps", 4, F32, 512)
        ps_o = B.psring(es, "cpo", 2, F32, 512)
        ps_t = B.psring(es, "cpt", 1, BF16, 1024)
        ps_n = B.psring(es, "cpn", 1, F32, 512)
        for s in range(NS):
            for hp in range(4):
                outs = []
                for which in range(2):
                    rw_, rwB = raw.next()
                    row0 = (15 + 4 * which + hp) * 128
                    B.load(rw_[:], PT[row0:row0 + 128, s * L:(s + 1) * L], (PTB,), (rwB,))
                    B.act(sq[:], rw_[:], AF.Square, (rwB,), (sqB,))
                    dst, dstB = (qn_r if which == 0 else kn_r).next()
                    for j in range(NP):
                        ps, psB = ps_n.next()
                        B.mm(ps[:], bdb[:], sq[:, j * 512:(j + 1) * 512], True, True, (bdbB, sqB), (psB,))
                        rr, rrB = rin.next()
                        if which == 0:
                            B.ts("dve", rr[:], ps[:], 1.0, 64e-6, ALU.mult, ALU.add, (psB,), (rrB,))
                        else:
                            B.ts("dve", rr[:], ps[:], 1.0 / 64, 1e-6, ALU.mult, ALU.add, (psB,), (rrB,))
                        B.act(rr[:], rr[:], AF.Sqrt, (rrB,), (rrB,))
                        B.S.op("dve", lambda e, o=rr[:], a=rr[:]: e.reciprocal(out=o, in_=a), (rrB,), (rrB,))
                        B.stt("dve", dst[:, j * 512:(j + 1) * 512], rw_[:, j * 512:(j + 1) * 512], qkg[:, which:which + 1], rr[:], ALU.mult, ALU.mult,
                              (rwB, qkgB, rrB), (dstB,))
                    outs.append((dst, dstB))
                (qn, qnB), (kn, knB) = outs
                vev, vevB = vev_r.next()
                vod, vodB = vod_r.next()
                st_, stB_ = vst.next()
                B.load(st_[:], VT[s * L:(s + 1) * L, hp * 128:(hp + 1) * 128].rearrange("(j p) f -> p j f", p=128), (VTB,), (stB_,))
                B.cp("pool", vev[:, :, :, 0:64], st_[:].rearrange("p j (e d) -> p j e d", e=2), (stB_,), (vevB,))
                st2, stB2 = vst.next()
                B.load(st2[:, 0:NC - 1, :], VT[s * L + 64:(s + 1) * L - 64, hp * 128:(hp + 1) * 128].rearrange("(j p) f -> p j f", p=128), (VTB,), (stB2,))
                B.cp("pool", vod[:, 0:NC - 1, :, 0:64], st2[:, 0:NC - 1, :].rearrange("p j (e d) -> p j e d", e=2), (stB2,), (vodB,))
                NPAIR = ROWS // 2
                stq = {}
                stp = {}
                ptst = {"pt": None}

                def stage_qk(r2):
                    banks = [ps_s.next(), ps_s.next()]
                    pbs = [pb_r.next(), pb_r.next()]
                    info = []
                    for rr in range(2):
                        r = 2 * r2 + rr
                        rs = min(max(r - 4, 0), ROWS - 8)
                        info.append((r, rs, rs - r + 7))
                    for e in range(2):
                        ps, psB = banks[e]
                        psv = ps[:].rearrange("p (a j c) -> p a j c", a=2, j=4)
                        for rr in range(2):
                            r, rs, dr0 = info[rr]
                            for j in range(4):
                                k0 = (rs + 2 * j) * 64
                                B.mm(psv[:, rr, j, :], kn[64 * e:64 * e + 64, k0:k0 + 128], qn[64 * e:64 * e + 64, r * 64:(r + 1) * 64], True, True, (knB, qnB), (psB,))
                        pb, pbB = pbs[e]
                        B.act(pb[:].rearrange("p a j c -> p (a j c)"), ps[:], AF.Exp, (psB,), (pbB,))
                        for rr in range(2):
                            r, rs, dr0 = info[rr]
                            B.tt("dve", pb[:, rr], pb[:, rr], EB[:, 2 * hp + e, dr0:dr0 + 7:2, :], ALU.mult, (pbB, EBB), (pbB,))
                    stq[r2] = (pbs, info)

                def stage_pv(r2):
                    pbs, info = stq.pop(r2)
                    nas = []
                    for rr in range(2):
                        r, rs, dr0 = info[rr]
                        po, poB = ps_o.next()
                        for e in range(2):
                            pb, pbB = pbs[e]
                            for j in range(4):
                                kr = rs + 2 * j
                                vsrc, vB_ = (vev, vevB) if kr % 2 == 0 else (vod, vodB)
                                B.mm(po[0:64, e * 65:(e + 1) * 65], pb[:, rr, j, :], vsrc[:, kr // 2, e, :], j == 0, j == 3, (pbB, vB_), (poB,))
                        pov = po[0:64, 0:130].rearrange("p (e d) -> p e d", e=2)
                        rc, rcB = rec_r.next()
                        B.S.op("dve", lambda e_, o=rc[:], a=pov[:, :, 64]: e_.reciprocal(out=o, in_=a), (poB,), (rcB,))
                        na, naB = na_r.next()
                        B.tt("dve", na[:].rearrange("p (e d) -> p e d", e=2), pov[:, :, 0:64], rc[:].unsqueeze(2).broadcast_to([64, 2, 64]), ALU.mult, (poB, rcB), (naB,))
                        nas.append((r, na, naB))
                    stp[r2] = nas

                def stage_tr(r2):
                    for (r, na, naB) in stp.pop(r2):
                        if r % 8 == 0:
                            ptst["pt"] = ps_t.next()
                        pt, ptB = ptst["pt"]
                        B.tr(pt[:, (r % 8) * 64:(r % 8 + 1) * 64], na[:], identb[0:64, 0:64], (naB, identB), (ptB,))
                        if r % 8 == 7:
                            nt, ntB = nat_r.next()
                            B.cp("act", nt[:], pt[:, 0:512], (ptB,), (ntB,))
                            c0 = s * L + (r - 7) * 64
                            B.store(MX[(4 + hp) * 128:(5 + hp) * 128, c0:c0 + 512], nt[:], (ntB,), (MXB,))

                for i in range(NPAIR + 2):
                    if i < NPAIR:
                        stage_qk(i)
                    if 0 <= i - 1 < NPAIR:
                        stage_pv(i - 1)
                    if 0 <= i - 2 < NPAIR:
                        stage_tr(i - 2)
        S.flush()


def phase_B(B, li, env):
    S = B.S
    L, NS, T, NC = B.L, B.NS, B.T, B.L // 128
    NP = L // 512
    P = env["P"]
    PT, PTB, MX, MXB, LR, LRB = env["PT"], env["PTB"], env["MX"], env["MXB"], env["LR"], env["LRB"]
    SH, YD = env["SH"], env["YD"]
    SHc, SHcB = SH[li]
    identb, identB = env["identb"], env["identB"]
    mu, muB, cmid, cmidB = env["mu"], env["muB"], env["cmid"], env["cmidB"]

    def cl(es, name, cols, lo, dt=F32):
        t, b = B.sb(es, name, [128, cols], F32)
        B.load(t[:], P[name][:, lo:lo + cols], (), (b,))
        if dt == BF16:
            t2, b2 = B.sb(es, name + "b", [128, cols], BF16)
            B.cp("dve", t2[:], t[:], (b,), (b2,))
            return t2, b2
        return t, b

    with ExitStack() as es:
        raw = B.sbring(es, "braw", [128, L + 2], BF16, 3)
        for (t, b) in raw.items:
            B.memset("pool", t[:], 0.0, (b,))
        tmp = B.sbring(es, "btmp", [128, L], BF16, 2)
        outr = B.sbring(es, "bout", [128, L], BF16, 3)
        vall, vallB = B.sb(es, "vall", [128, 4, L], BF16)
        if li > 0:
            vdown, vdownB = cl(es, "vdown", 128, 0, BF16)
            vupf, vupfB = B.sb(es, "vupf", [32, 512], F32)
            B.load(vupf[:], P["vup"][:, :], (), (vupfB,))
            vup, vupB = B.sb(es, "vupb", [32, 512], BF16)
            B.cp("dve", vup[:], vupf[:], (vupfB,), (vupB,))
            v0f, v0fB = cl(es, "v0f", 4, 0)
            tb_r = B.sbring(es, "vtb", [32, 512], BF16, 2)
            mix_r = B.sbring(es, "vmix", [128, 512], BF16, 2)
            vf_r = B.sbring(es, "vvf", [128, 512], BF16, 2)
            dd_r = B.sbring(es, "vdd", [128, 512], BF16, 2)
            psv = B.psring(es, "bpsv", 4, F32, 512)
        for s in range(NS):
            for c in range(15):
                rw_, rwB = raw.next()
                B.load(rw_[:, 1:L + 1], PT[c * 128:(c + 1) * 128, s * L:(s + 1) * L], (PTB,), (rwB,))
                t1, t1B = tmp.next()
                m0 = mu[:, li * 30 + c:li * 30 + c + 1]
                m1 = mu[:, li * 30 + 15 + c:li * 30 + 16 + c]
                cm = cmid[:, li * 15 + c:li * 15 + c + 1]
                B.ts("dve", t1[:], rw_[:, 0:L], m0, None, ALU.mult, None, (rwB, muB), (t1B,))
                B.stt("dve", t1[:], rw_[:, 2:L + 2], m1, t1[:], ALU.mult, ALU.add, (rwB, muB, t1B), (t1B,))
                if 8 <= c < 12:
                    B.stt("dve", vall[:, c - 8, :], rw_[:, 1:L + 1], cm, t1[:], ALU.mult, ALU.add, (rwB, cmidB, t1B), (vallB,))
                    continue
                o, oB = outr.next()
                B.stt("dve", o[:], rw_[:, 1:L + 1], cm, t1[:], ALU.mult, ALU.add, (rwB, cmidB, t1B), (oB,))
                if c < 8:
                    B.store(SHc[c * 128:(c + 1) * 128, s * L:(s + 1) * L], o[:], (oB,), (SHcB,))
                else:
                    if c == 12:
                        B.act(o[:], o[:], AF.Tanh, (oB,), (oB,))
                    elif c == 14:
                        B.act(o[:], o[:], AF.Sigmoid, (oB,), (oB,))
                    B.store(LR[(c - 12) * 128:(c - 11) * 128, s * L:(s + 1) * L], o[:], (oB,), (LRB,))
            if li > 0:
                SH0, SH0B = SH[0]
                for j in range(NP):
                    pt_, ptB_ = psv.next()
                    for hp in range(4):
                        B.mm(pt_[0:32, :], vdown[:, hp * 32:(hp + 1) * 32], vall[:, hp, j * 512:(j + 1) * 512], hp == 0, hp == 3, (vdownB, vallB), (ptB_,))
                    tb, tbB = tb_r.next()
                    B.cp("act", tb[:], pt_[0:32, :], (ptB_,), (tbB,))
                    for hp in range(4):
                        pm, pmB = psv.next()
                        B.mm(pm[:], vup[0:32, hp * 128:(hp + 1) * 128], tb[:], True, True, (vupB, tbB), (pmB,))
                        mx_, mxB_ = mix_r.next()
                        B.act(mx_[:], pm[:], AF.Sigmoid, (pmB, v0fB), (mxB_,), bias=v0f[:, hp:hp + 1])
                        vf, vfB = vf_r.next()
                        B.load(vf[:], SH0[(8 + hp) * 128:(9 + hp) * 128, s * L + j * 512:s * L + (j + 1) * 512], (SH0B,), (vfB,))
                        dd, ddB = dd_r.next()
                        vsl = vall[:, hp, j * 512:(j + 1) * 512]
                        B.tt("dve", dd[:], vf[:], vsl, ALU.subtract, (vfB, vallB), (ddB,))
                        B.tt("dve", dd[:], dd[:], mx_[:], ALU.mult, (ddB, mxB_), (ddB,))
                        B.tt("dve", vsl, vsl, dd[:], ALU.add, (vallB, ddB), (vallB,))
            for hp in range(4):
                B.store(SHc[(8 + hp) * 128:(9 + hp) * 128, s * L:(s + 1) * L], vall[:, hp, :], (vallB,), (SHcB,))
        S.flush()

    NK, NKB = env["NK"], env["NKB"]
    KBs, KBsB = env["KBs"], env["KBsB"]
    VK, VKB = env["VK"], env["VKB"]
    SW, SWB = env["SW"], env["SWB"]
    with ExitStack() as eo:
        dup, dupB = cl(eo, "dup", 1024, li * 1024, BF16)
        aup, aupB = cl(eo, "aup", 1024, li * 1024, BF16)
        a0f, a0fB = cl(eo, "a0f", 8, li * 8)
        kkf, kkfB = cl(eo, "kkf", 4, li * 4)
        kaf, kafB = cl(eo, "kaf", 4, li * 4)
        omka, omkaB = B.sb(eo, "omka", [128, 4], F32)
        B.ts("dve", omka[:], kaf[:], -1.0, 1.0, ALU.mult, ALU.add, (kafB,), (omkaB,))
        w0bc, w0bcB = cl(eo, "w0row", 1024, li * 1024)
        bdb, bdbB = cl(eo, "bdones", 128, 0, BF16)
        ks_r = B.sbring(eo, "ksp", [128, 512], BF16, 3)
        vs_r = B.sbring(eo, "vsp", [128, 512], BF16, 3)
        tw_r = B.sbring(eo, "twp", [128, 512], BF16, 3)
        ad_r = B.sbring(eo, "adp", [128, 512], BF16, 3)
        ap_r = B.sbring(eo, "app", [128, 512], BF16, 3)
        kk_r = B.sbring(eo, "kkp", [128, 512], BF16, 3)
        sq_r = B.sbring(eo, "sqp", [128, 512], BF16, 3)
        nk_r = B.sbring(eo, "nkp", [128, 512], BF16, 3)
        kb_r = B.sbring(eo, "kbp", [128, 2, 2, 512], BF16, 3)
        rin = B.sbring(eo, "rin", [128, 512], F32, 3)
        swp = B.sbring(eo, "swp", [128, 512], F32, 3)
        swo = B.sbring(eo, "swo", [128, 2, 2, 128], F32, 3)
        vto = B.sbring(eo, "vto", [128, 4, 128], BF16, 3)
        ps = B.psring(eo, "b2aps", 6, F32, 512)
        pst = B.psring(eo, "b2apt", 2, BF16, 1024)
        for s in range(NS):
            for hp in range(4):
                for j in range(NP):
                    g0 = s * L + j * 512
                    pc_ = slice(g0, g0 + 512)
                    rows = slice(hp * 128, (hp + 1) * 128)
                    ksp, kspB = ks_r.next()
                    vsp, vspB = vs_r.next()
                    twp, twpB = tw_r.next()
                    adp, adpB = ad_r.next()
                    B.load(ksp[:], SHc[(4 + hp) * 128:(5 + hp) * 128, pc_], (SHcB,), (kspB,))
                    B.load(vsp[:], SHc[(8 + hp) * 128:(9 + hp) * 128, pc_], (SHcB,), (vspB,))
                    B.load(twp[:], LR[0:128, pc_], (LRB,), (twpB,))
                    B.load(adp[:], LR[128:256, pc_], (LRB,), (adpB,))
                    kkp, kkpB = kk_r.next()
                    sqp, sqpB = sq_r.next()
                    B.act(kkp[:], ksp[:], AF.Copy, (kspB, kkfB), (kkpB,), scale=kkf[:, hp:hp + 1])
                    B.act(sqp[:], kkp[:], AF.Square, (kkpB,), (sqpB,))
                    pn, pnB = ps.next()
                    B.mm(pn[:], bdb[:], sqp[:], True, True, (bdbB, sqpB), (pnB,))
                    rr, rrB = rin.next()
                    B.ts("dve", rr[:], pn[:], 1e-12, None, ALU.max, None, (pnB,), (rrB,))
                    B.act(rr[:], rr[:], AF.Sqrt, (rrB,), (rrB,))
                    B.S.op("dve", lambda e_, o=rr[:], a=rr[:]: e_.reciprocal(out=o, in_=a), (rrB,), (rrB,))
                    nkp, nkpB = nk_r.next()
                    B.stt("dve", nkp[:], kkp[:], -1.0, rr[:], ALU.mult, ALU.mult, (kkpB, rrB), (nkpB,))
                    B.store(NK[rows, pc_], nkp[:], (nkpB,), (NKB,))
                    kbp, kbpB = kb_r.next()
                    for d in range(2):
                        pa, paB = ps.next()
                        B.mm(pa[:], aup[:, d * 512 + hp * 128:d * 512 + (hp + 1) * 128], adp[:], True, True, (aupB, adpB), (paB,))
                        ap_, apB = ap_r.next()
                        B.act(ap_[:], pa[:], AF.Sigmoid, (paB, a0fB), (apB,), bias=a0f[:, d * 4 + hp:d * 4 + hp + 1])
                        B.ts("dve", kbp[:, d, 0, :], ap_[:], kaf[:, hp:hp + 1], omka[:, hp:hp + 1], ALU.mult, ALU.add, (apB, kafB, omkaB), (kbpB,))
                        B.tt("dve", kbp[:, d, 0, :], kbp[:, d, 0, :], ksp[:], ALU.mult, (kbpB, kspB), (kbpB,))
                        B.stt("dve", kbp[:, d, 1, :], nkp[:], -1.0, ap_[:], ALU.mult, ALU.mult, (nkpB, apB), (kbpB,))
                    for d in range(2):
                        B.store(KBs[d, :, rows, pc_].rearrange("k p t -> p k t"), kbp[:, d, :, :], (kbpB,), (KBsB,))
                    for c2 in range(2):
                        pw, pwB = ps.next()
                        for cc in range(2):
                            for d in range(2):
                                B.mm(pw[:, (cc * 2 + d) * 128:(cc * 2 + d + 1) * 128], twp[:, (2 * c2 + cc) * 128:(2 * c2 + cc + 1) * 128],
                                     dup[:, d * 512 + hp * 128:d * 512 + (hp + 1) * 128], True, True, (twpB, dupB), (pwB,))
                        sw, swB = swp.next()
                        w0v = w0bc[:].rearrange("p (d f) -> p d f", d=2)[:, :, hp * 128:(hp + 1) * 128]
                        B.tt("dve", sw[:].rearrange("p (c d f) -> p c d f", c=2, d=2), pw[:].rearrange("p (c d f) -> p c d f", c=2, d=2),
                             w0v.unsqueeze(1).broadcast_to([128, 2, 2, 128]), ALU.add, (pwB, w0bcB), (swB,))
                        so, soB = swo.next()
                        B.act(so[:].rearrange("p c d f -> p (c d f)"), sw[:], AF.Sigmoid, (swB,), (soB,))
                        t0 = g0 + c2 * 256
                        for d in range(2):
                            B.store(SW[d, t0:t0 + 256, rows].rearrange("(c p) f -> p c f", p=128), so[:, :, d, :], (soB,), (SWB,))
                    pt_, ptB_ = pst.next()
                    for cc in range(4):
                        B.tr(pt_[:, cc * 128:(cc + 1) * 128], vsp[:, cc * 128:(cc + 1) * 128], identb[:], (vspB, identB), (ptB_,))
                    vo, voB = vto.next()
                    B.cp("act", vo[:].rearrange("p c f -> p (c f)"), pt_[:, 0:512], (ptB_,), (voB,))
                    B.store(VK[pc_, rows].rearrange("(c p) f -> p c f", p=128), vo[:], (voB,), (VKB,))
        S.flush()

    with ExitStack() as eo:
        tri, triB = cl(eo, "tri", 512, 0)
        msk, mskB = cl(eo, "masks", 512, 0)
        NU = 8
        def rg(n, shape, dt, k):
            return B.sbring(eo, n, shape, dt, k)
        i_rs = rg("i_rs", [128, 128], BF16, 2 * NU)
        i_nk = rg("i_nk", [128, 128], BF16, 2 * NU)
        i_kb = rg("i_kb", [128, 2, 128], BF16, 2 * NU)
        i_vt = rg("i_vt", [128, 128], BF16, 2 * NU)
        i_sw = rg("i_sw", [128, 128], F32, 2 * NU)
        E_r = rg("E", [128, 3, 128], F32, 2 * NU)
        ar_r = rg("ar", [128, 2, 128], BF16, 2 * NU)
        kbt_r = rg("kbt", [128, 2, 128], BF16, 2 * NU)
        MA_r = rg("MA", [128, 2, 2, 128], BF16, 2 * NU)
        RBK_r = rg("RBK", [128, 2, 2, 128], BF16, 2 * NU)
        Wf_r = rg("Wf", [128, 2, 128], BF16, 2 * NU)
        ktb_r = rg("ktb", [128, 2, 128], BF16, NU + 2)
        Ap_r = rg("Ap", [128, 2, 2, 128], BF16, NU // 2 + 1)
        R_r = rg("R", [128, 2, 2, 128], BF16, 2 * NU)
        PA_r = rg("PA", [128, 2, 2, 128], BF16, NU)
        X_r = rg("Xsb", [128, 128], BF16, NU)
        U_r = rg("Usb", [128, 128], BF16, NU)
        Y_r = rg("ysb", [128, 128], F32, NU)
        Hg_r = rg("Hg", [128, 64], F32, NU)
        Hst = {(hp, d): B.sb(eo, "Hst", [128, 64], F32) for hp in range(4) for d in range(2)}
        Hbd = {(hp, d): B.sb(eo, "Hbd", [128, 128], BF16) for hp in range(4) for d in range(2)}
        ps = B.psring(eo, "bps", 8, F32, 512)
        MI = {0: (0, 1, 2), 1: (1, 0, 3)}
        evq = ["act", "dve"]

        def mk(i):
            return msk[:, i * 128:(i + 1) * 128]

        def bc2(ap, n=2):
            return ap.unsqueeze(1).broadcast_to([128, n, 128])

        def groups(units, g):
            return [units[k:k + g] for k in range(0, len(units), g)]

        def v4(t_):
            return t_[:].rearrange("p (a t) -> p a t", a=4)

        def prep(units, s, st):
            for un in units:
                hp, d, c = un
                t0 = s * L + c * 128
                rows = slice(hp * 128, (hp + 1) * 128)
                u = {}
                u["rs"] = i_rs.next()
                u["nk"] = i_nk.next()
                u["kb"] = i_kb.next()
                u["vt"] = i_vt.next()
                u["sw"] = i_sw.next()
                B.load(u["rs"][0][:], SHc[rows, t0:t0 + 128], (SHcB,), (u["rs"][1],))
                B.load(u["nk"][0][:], NK[rows, t0:t0 + 128], (NKB,), (u["nk"][1],))
                B.load(u["kb"][0][:], KBs[d, :, rows, t0:t0 + 128].rearrange("k p t -> p k t"), (KBsB,), (u["kb"][1],))
                B.load(u["vt"][0][:], VK[t0:t0 + 128, rows], (VKB,), (u["vt"][1],))
                B.load(u["sw"][0][:], SW[d, t0:t0 + 128, rows], (SWB,), (u["sw"][1],))
                st[un] = u
            for grp in groups(units, 4):
                bk = [ps.next(), ps.next()]
                for k_, un in enumerate(grp):
                    hp, d, c = un
                    u = st[un]
                    pc, pcB = bk[k_ // 2]
                    o = (k_ % 2) * 256
                    sw, swB = u["sw"]
                    B.mm(pc[:, o:o + 256], sw[:], tri[:, (2 * d) * 128:(2 * d + 2) * 128], True, True, (swB, triB), (pcB,))
                for k_, un in enumerate(grp):
                    u = st[un]
                    pc, pcB = bk[k_ // 2]
                    o = (k_ % 2) * 256
                    E, EB_ = E_r.next()
                    B.act(E[:, 0:2, :].rearrange("p a t -> p (a t)"), pc[:, o:o + 256], AF.Exp, (pcB,), (EB_,))
                    B.act(E[:, 2, :], pc[:, o:o + 128], AF.Exp, (pcB,), (EB_,), scale=-1.0)
                    u["E"] = (E, EB_)
            for un in units:
                u = st[un]
                E, EB_ = u["E"]
                rs_, rsB = u["rs"]
                nk, nkB = u["nk"]
                kb, kbB = u["kb"]
                ar, arB = ar_r.next()
                ktb, ktbB = ktb_r.next()
                B.tt("dve", ar[:, 0, :], nk[:], E[:, 1, :], ALU.mult, (nkB, EB_), (arB,))
                B.tt("pool", ar[:, 1, :], rs_[:], E[:, 0, :], ALU.mult, (rsB, EB_), (arB,))
                B.tt("pool" if (un[0] % 2 == 0) else "dve", ktb[:], kb[:], bc2(E[:, 2, :]), ALU.mult, (kbB, EB_), (ktbB,))
                u["ar"], u["ktb"] = (ar, arB), (ktb, ktbB)
            yield
            for pair in groups(units, 2):
                yield
                C = [ps.next(), ps.next()]
                Ap, ApB = Ap_r.next()
                hb = {}
                for k_, un in enumerate(pair):
                    u = st[un]
                    ar, arB = u["ar"]
                    ktb, ktbB = u["ktb"]
                    arf = ar[:].rearrange("p a t -> p (a t)")
                    for e in range(2):
                        hs = slice(64 * e, 64 * e + 64)
                        bk, bkB = ps.next()
                        hb[(un, e)] = (bk, bkB)
                        B.mm(bk[:, 0:256], ktb[hs, 1, :], arf[hs, :], True, True, (ktbB, arB), (bkB,))
                        B.mm(bk[:, 256:512], ktb[hs, 0, :], arf[hs, :], True, True, (ktbB, arB), (bkB,))
                        cb, cbB = C[e]
                        B.mm(cb[:, k_ * 128:(k_ + 1) * 128], ar[hs, 0, :], ktb[hs, 1, :], True, True, (arB, ktbB), (cbB,))
                    cb, cbB = C[k_]
                    for kb_ in range(2):
                        B.mm(cb[:, 256 + kb_ * 128:256 + (kb_ + 1) * 128], ktb[:, kb_, :], identb[:], True, True, (ktbB, identB), (cbB,))
                for k_, un in enumerate(pair):
                    hp, d, c = un
                    u = st[un]
                    mi = MI[d]
                    MA, MAB = MA_r.next()
                    RBK, RBKB = RBK_r.next()
                    for e in range(2):
                        bk, bkB = hb[(un, e)]
                        B.tt("dve", MA[:, e, :, :], v4(bk)[:, 0:4:2, :], bc2(mk(mi[0])), ALU.mult, (bkB, mskB), (MAB,))
                        B.tt("dve", RBK[:, e, :, :], v4(bk)[:, 1:4:2, :], bc2(mk(mi[2])), ALU.mult, (bkB, mskB), (RBKB,))
                    cb, cbB = C[k_]
                    kbt, kbtB = kbt_r.next()
                    B.cp("dve", kbt[:].rearrange("p a b -> p (a b)"), cb[:, 256:512], (cbB,), (kbtB,))
                    u.update(MA=(MA, MAB), RBK=(RBK, RBKB), kbt=(kbt, kbtB), Ap=(Ap, ApB, k_))
                d0 = pair[0][1]
                for e in range(2):
                    cb, cbB = C[e]
                    B.tt("dve", Ap[:, e, :, :], cb[:, 0:256].rearrange("p (a t) -> p a t", a=2), bc2(mk(MI[d0][1])), ALU.mult, (cbB, mskB), (ApB,))
            for un in units:
                u = st[un]
                MA, MAB = u["MA"]
                Ap, ApB, k_ = u["Ap"]
                u["PM"] = lambda e, MA=MA: MA[:, e, 0, :]
                u["PMB"] = MAB
                u["PA"] = lambda e, Ap=Ap, k_=k_: Ap[:, e, k_, :]
                u["PAB"] = ApB
                u["Wl"] = None
            for lvl in range(7):
                yield
                for pair in groups(units, 2):
                    pab, pabB = ps.next()
                    lb = {}
                    for k_, un in enumerate(pair):
                        u = st[un]
                        lbk, lbkB = ps.next()
                        lb[un] = (lbk, lbkB)
                        PMB, PAB = u["PMB"], u["PAB"]
                        for e in range(2):
                            o = e * 256
                            PA, PM = u["PA"](e), u["PM"](e)
                            if lvl == 0:
                                B.mm(lbk[:, o:o + 128], PA, PM, True, True, (PAB, PMB), (lbkB,))
                                B.mm(lbk[:, o + 128:o + 256], PA, identb[:], True, False, (PAB, identB), (lbkB,))
                                B.mm(lbk[:, o + 128:o + 256], identb[:], identb[:], False, True, (identB,), (lbkB,))
                            else:
                                R, RB_ = u["Wl"]
                                if lvl < 6:
                                    B.mm(lbk[:, o:o + 256], PA, R[:, e, :, :].rearrange("p a t -> p (a t)"), True, False, (PAB, RB_), (lbkB,))
                                else:
                                    B.mm(lbk[:, o + 128:o + 256], PA, R[:, e, 1, :], True, False, (PAB, RB_), (lbkB,))
                                B.mm(lbk[:, o + 128:o + 256], identb[:], R[:, e, 1, :], False, True, (identB, RB_), (lbkB,))
                            if lvl < 6:
                                B.mm(pab[:, (k_ * 2 + e) * 128:(k_ * 2 + e + 1) * 128], PM, PA, True, True, (PAB, PMB), (pabB,))
                    PAn, PAnB = PA_r.next()
                    for k_, un in enumerate(pair):
                        u = st[un]
                        lbk, lbkB = lb[un]
                        q = evq[k_ % 2]
                        if lvl < 6:
                            R2, R2B = R_r.next()
                            B.cp(q, R2[:].rearrange("p e a t -> p (e a t)"), lbk[:], (lbkB,), (R2B,))
                            u["Wl"] = (R2, R2B)
                            u["PM"] = lambda e, R2=R2: R2[:, e, 0, :]
                            u["PMB"] = R2B
                            u["PA"] = lambda e, PAn=PAn, k_=k_: PAn[:, k_, e, :]
                            u["PAB"] = PAnB
                        else:
                            Wf, WfB = Wf_r.next()
                            B.cp(q, Wf[:], v4(lbk)[:, 1:4:2, :], (lbkB,), (WfB,))
                            u["W"] = (Wf, WfB)
                    if lvl < 6:
                        B.cp(evq[(lvl + 1) % 2], PAn[:].rearrange("p k e t -> p (k e t)"), pab[:], (pabB,), (PAnB,))

        def chains(units, st, s):
            for grp in groups(units, 4):
                bk = [ps.next(), ps.next(), ps.next(), ps.next()]
                for k_, un in enumerate(grp):
                    hp, d, c = un
                    u = st[un]
                    ar, arB = u["ar"]
                    MA, MAB = u["MA"]
                    vt, vtB = u["vt"]
                    Hb, HbB = Hbd[(hp, d)]
                    bx, bxB = bk[k_]
                    o = 0
                    B.mm(bx[:, o:o + 128], ar[:, 0, :], Hb[:], True, False, (arB, HbB), (bxB,))
                    for e in range(2):
                        B.mm(bx[:, o + e * 64:o + (e + 1) * 64], MA[:, e, 1, :], vt[:, e * 64:(e + 1) * 64], False, True, (MAB, vtB), (bxB,))
                for k_, un in enumerate(grp):
                    u = st[un]
                    bx, bxB = bk[k_]
                    o = 0
                    u["X"] = X_r.next()
                    B.cp(evq[k_ % 2], u["X"][0][:], bx[:, o:o + 128], (bxB,), (u["X"][1],))
            yield
            yield
            for grp in groups(units, 4):
                bk = [ps.next(), ps.next()]
                for k_, un in enumerate(grp):
                    u = st[un]
                    W, WB = u["W"]
                    X, XB = u["X"]
                    bu, buB = bk[k_ // 2]
                    o = (k_ % 2) * 128
                    for e in range(2):
                        B.mm(bu[:, o + e * 64:o + (e + 1) * 64], W[:, e, :], X[:, e * 64:(e + 1) * 64], True, True, (WB, XB), (buB,))
                for k_, un in enumerate(grp):
                    u = st[un]
                    bu, buB = bk[k_ // 2]
                    o = (k_ % 2) * 128
                    u["U"] = U_r.next()
                    B.cp(evq[(k_ // 2 + 1) % 2], u["U"][0][:], bu[:, o:o + 128], (buB,), (u["U"][1],))
            yield
            yield
            for grp in groups(units, 4):
                bk = [ps.next(), ps.next(), ps.next(), ps.next()]
                for k_, un in enumerate(grp):
                    hp, d, c = un
                    u = st[un]
                    ar, arB = u["ar"]
                    RBK, RBKB = u["RBK"]
                    U, UB = u["U"]
                    vt, vtB = u["vt"]
                    kbt, kbtB = u["kbt"]
                    Hb, HbB = Hbd[(hp, d)]
                    bo, boB = bk[k_ // 2]
                    bh, bhB = bk[2 + k_ // 2]
                    o = (k_ % 2) * 128
                    B.mm(bo[:, o:o + 128], ar[:, 1, :], Hb[:], True, False, (arB, HbB), (boB,))
                    for e in range(2):
                        B.mm(bo[:, o + e * 64:o + (e + 1) * 64], RBK[:, e, 0, :], U[:, e * 64:(e + 1) * 64], False, False, (RBKB, UB), (boB,))
                        B.mm(bo[:, o + e * 64:o + (e + 1) * 64], RBK[:, e, 1, :], vt[:, e * 64:(e + 1) * 64], False, True, (RBKB, vtB), (boB,))
                    B.mm(bh[:, o:o + 128], kbt[:, 1, :], U[:], True, False, (kbtB, UB), (bhB,))
                    B.mm(bh[:, o:o + 128], kbt[:, 0, :], vt[:], False, True, (kbtB, vtB), (bhB,))
                for k_, un in enumerate(grp):
                    hp, d, c = un
                    u = st[un]
                    bo, boB = bk[k_ // 2]
                    bh, bhB = bk[2 + k_ // 2]
                    o = (k_ % 2) * 128
                    E, EB_ = u["E"]
                    Hs, HsB = Hst[(hp, d)]
                    Hb, HbB = Hbd[(hp, d)]
                    gcol = 127 if d == 0 else 0
                    gam = E[:, 0, gcol:gcol + 1]
                    Hg, HgB = Hg_r.next()
                    B.ts("dve", Hg[:], Hs[:], gam, None, ALU.mult, None, (HsB, EB_), (HgB,))
                    for e in range(2):
                        hs = slice(64 * e, 64 * e + 64)
                        B.stt("dve", Hs[hs, :], bh[hs, o + e * 64:o + (e + 1) * 64], gam[hs, :], Hg[hs, :], ALU.mult, ALU.add, (bhB, EB_, HgB), (HsB,))
                        B.cp("act", Hb[hs, e * 64:(e + 1) * 64], Hs[hs, :], (HsB,), (HbB,))
                    ysb, ysbB = Y_r.next()
                    B.cp("act", ysb[:], bo[:, o:o + 128], (boB,), (ysbB,))
                    t0 = s * L + c * 128
                    B.store(YD[d][0][t0:t0 + 128, hp * 128:(hp + 1) * 128], ysb[:], (ysbB,), (YD[d][1],))

        def pipeline(unit_fn, s):
            st = {}
            yield from prep(unit_fn(0), s, st)
            for i in range(NC):
                cur = st
                gens = [chains(unit_fn(i), cur, s)]
                if i + 1 < NC:
                    st = {}
                    gens.append(prep(unit_fn(i + 1), s, st))
                while gens:
                    for g in list(gens):
                        try:
                            next(g)
                        except StopIteration:
                            gens.remove(g)
                    yield

        STAGGER = 6
        for s in range(NS if env.get("do_scan", True) else 0):
            for key in Hst:
                B.memset("pool", Hst[key][0][:], 0.0, (Hst[key][1],))
                B.memset("pool", Hbd[key][0][:], 0.0, (Hbd[key][1],))
            pA = pipeline(lambda i: [(hp, 0, i) for hp in range(4)], s)
            pB = pipeline(lambda i: [(hp, 1, NC - 1 - i) for hp in range(4)], s)
            live = [pA]
            tick = 0
            started_b = False
            while live:
                for g in list(live):
                    try:
                        next(g)
                    except StopIteration:
                        live.remove(g)
                tick += 1
                if tick == STAGGER and not started_b:
                    live.append(pB)
                    started_b = True
        S.flush()
    phase_B3(B, li, env)


def phase_B3(B, li, env):
    S = B.S
    L, NS, T, NC = B.L, B.NS, B.T, B.L // 128
    P = env["P"]
    MX, MXB, LR, LRB = env["MX"], env["MXB"], env["LR"], env["LRB"]
    SHc, SHcB = env["SH"][li]
    YD = env["YD"]
    identb, identB = env["identb"], env["identB"]
    with ExitStack() as es:
        def cl(name, cols, lo, dt=F32):
            t, b = B.sb(es, name, [128, cols], F32)
            B.load(t[:], P[name][:, lo:lo + cols], (), (b,))
            if dt == BF16:
                t2, b2 = B.sb(es, name + "b", [128, cols], BF16)
                B.cp("dve", t2[:], t[:], (b,), (b2,))
                return t2, b2
            return t, b
        aup, aupB = cl("aup", 1024, li * 1024, BF16)
        gup, gupB = cl("gup", 512, li * 512, BF16)
        a0bc, a0bcB = cl("a0row", 1024, li * 1024)
        kabc, kabcB = cl("karow", 512, li * 512)
        rkbcf, rkbcfB = cl("rkrow", 512, li * 512)
        lgbc, lgbcB = cl("lgrow", 512, li * 512)
        lbbc, lbbcB = cl("lbrow", 512, li * 512)
        omka2, omka2B = B.sb(es, "omka2", [128, 512], BF16)
        B.ts("dve", omka2[:], kabc[:], -2.0, 2.0, ALU.mult, ALU.add, (kabcB,), (omka2B,))
        kab, kabB = B.sb(es, "kab16", [128, 512], BF16)
        B.cp("dve", kab[:], kabc[:], (kabcB,), (kabB,))
        rkbc, rkbcB = B.sb(es, "rkb16", [128, 512], BF16)
        B.cp("dve", rkbc[:], rkbcf[:], (rkbcfB,), (rkbcB,))
        y_r = B.sbring(es, "py", [128, 2, 512], F32, 3)
        f_r = B.sbring(es, "pf", [128, 12, 128], BF16, 3)
        l_r = B.sbring(es, "pl", [128, 2, 128], BF16, 3)
        tk_r = B.sbring(es, "ptk", [128, 3, 512], BF16, 3)
        a_r = B.sbring(es, "pa", [128, 2, 512], BF16, 3)
        t_r = B.sbring(es, "ptt", [128, 512], BF16, 2)
        c_r = B.sbring(es, "pcn", [128, 512], F32, 2)
        q_r = B.sbring(es, "psq", [128, 512], F32, 2)
        s_r = B.sbring(es, "pst8", [128, 32], F32, 2)
        o_r = B.sbring(es, "pob", [128, 512], BF16, 2)
        m_r = B.sbring(es, "pmx", [128, 4, 128], BF16, 2)
        ps = B.psring(es, "pps", 5, F32, 512)
        pst = B.psring(es, "ppt", 3, BF16, 1024)
        h8 = lambda ap: ap.rearrange("p (h n) -> p h n", h=8)
        tiles = [(s, i) for s in range(NS) for i in range(NC)]
        b3 = {}
        wgen = None
        if li == 0 or li + 1 < B.depth:
            wstf = B.sbring(es, "wstf", [128, 2 * DFF], F32, 2)
            wstb = B.sbring(es, "wstb", [128, 2 * DFF], BF16, 2)

            def _bg():
                if li == 0:
                    yield from weight_conv_gen(B, env, 0, wstf, wstb, "pool", parts=(1, 2, 3))
                if li + 1 < B.depth:
                    yield from weight_conv_gen(B, env, li + 1, wstf, wstb, "pool")
            wgen = _bg()

        def stage1(s, i):
            t0 = s * L + i * 128
            yy, yyB = y_r.next()
            for d in range(2):
                B.load(yy[:, d, :], YD[d][0][t0:t0 + 128, :], (YD[d][1],), (yyB,))
            fm, fmB = f_r.next()
            B.load(fm[:], SHc[:, t0:t0 + 128].rearrange("(k p) t -> p k t", p=128), (SHcB,), (fmB,))
            lr, lrB = l_r.next()
            B.load(lr[:], LR[128:384, t0:t0 + 128].rearrange("(k p) t -> p k t", p=128), (LRB,), (lrB,))
            pa_, paB_ = pst.next()
            pb_, pbB_ = pst.next()
            for k in range(8):
                B.tr(pa_[:, k * 128:(k + 1) * 128], fm[:, k, :], identb[:], (fmB, identB), (paB_,))
            for k in range(4):
                B.tr(pb_[:, k * 128:(k + 1) * 128], fm[:, 8 + k, :], identb[:], (fmB, identB), (pbB_,))
            tk, tkB = tk_r.next()
            B.cp("act", tk[:, 0:2, :].rearrange("p a f -> p (a f)"), pa_[:], (paB_,), (tkB,))
            B.cp("act", tk[:, 2, :], pb_[:, 0:512], (pbB_,), (tkB,))
            aa, aaB = a_r.next()
            for d in range(2):
                pq, pqB = ps.next()
                B.mm(pq[:], lr[:, 0, :], aup[:, d * 512:(d + 1) * 512], True, True, (lrB, aupB), (pqB,))
                B.tt("dve", aa[:, d, :], pq[:], a0bc[:, d * 512:(d + 1) * 512], ALU.add, (pqB, a0bcB), (aaB,))
            B.act(aa[:].rearrange("p a f -> p (a f)"), aa[:].rearrange("p a f -> p (a f)"), AF.Sigmoid, (aaB,), (aaB,))
            b3[(s, i)] = (t0, yy, yyB, tk, tkB, aa, aaB, lr, lrB)

        def stage2(s, i):
            t0, yy, yyB, tk, tkB, aa, aaB, lr, lrB = b3.pop((s, i))
            t1, t1B = t_r.next()
            B.tt("dve", t1[:], aa[:, 0, :], aa[:, 1, :], ALU.add, (aaB,), (t1B,))
            B.tt("dve", t1[:], t1[:], kab[:], ALU.mult, (t1B, kabB), (t1B,))
            B.tt("dve", t1[:], t1[:], omka2[:], ALU.add, (t1B, omka2B), (t1B,))
            B.tt("dve", t1[:], t1[:], tk[:, 1, :], ALU.mult, (t1B, tkB), (t1B,))
            B.tt("dve", t1[:], t1[:], tk[:, 0, :], ALU.mult, (t1B, tkB), (t1B,))
            B.tt("dve", t1[:], t1[:], rkbc[:], ALU.mult, (t1B, rkbcB), (t1B,))
            st, stB = s_r.next()
            B.S.op("dve", lambda e_, o=st[:, 0:8], a=h8(t1[:]): e_.reduce_sum(out=o, in_=a, axis=AX.X), (t1B,), (stB,))
            cn, cnB = c_r.next()
            B.tt("dve", cn[:], yy[:, 0, :], yy[:, 1, :], ALU.add, (yyB,), (cnB,))
            B.S.op("dve", lambda e_, o=st[:, 8:16], a=h8(cn[:]): e_.reduce_sum(out=o, in_=a, axis=AX.X), (cnB,), (stB,))
            B.ts("dve", st[:, 8:16], st[:, 8:16], -1.0 / 64, None, ALU.mult, None, (stB,), (stB,))
            B.tt("dve", h8(cn[:]), h8(cn[:]), st[:, 8:16].unsqueeze(2).broadcast_to([128, 8, 64]), ALU.add, (cnB, stB), (cnB,))
            sqv, sqvB = q_r.next()
            B.tt("dve", sqv[:], cn[:], cn[:], ALU.mult, (cnB,), (sqvB,))
            B.S.op("dve", lambda e_, o=st[:, 16:24], a=h8(sqv[:]): e_.reduce_sum(out=o, in_=a, axis=AX.X), (sqvB,), (stB,))
            B.ts("dve", st[:, 16:24], st[:, 16:24], 1.0 / 64, 64e-5, ALU.mult, ALU.add, (stB,), (stB,))
            B.act(st[:, 16:24], st[:, 16:24], AF.Sqrt, (stB,), (stB,))
            B.S.op("dve", lambda e_, o=st[:, 24:32], a=st[:, 16:24]: e_.reciprocal(out=o, in_=a), (stB,), (stB,))
            B.tt("dve", h8(cn[:]), h8(cn[:]), st[:, 24:32].unsqueeze(2).broadcast_to([128, 8, 64]), ALU.mult, (cnB, stB), (cnB,))
            B.tt("dve", cn[:], cn[:], lgbc[:], ALU.mult, (cnB, lgbcB), (cnB,))
            B.tt("dve", cn[:], cn[:], lbbc[:], ALU.add, (cnB, lbbcB), (cnB,))
            B.tt("dve", h8(sqv[:]), h8(tk[:, 2, :]), st[:, 0:8].unsqueeze(2).broadcast_to([128, 8, 64]), ALU.mult, (tkB, stB, sqvB), (sqvB,))
            B.tt("dve", cn[:], cn[:], sqv[:], ALU.add, (cnB, sqvB), (cnB,))
            pg, pgB = ps.next()
            B.mm(pg[:], lr[:, 1, :], gup[:], True, True, (lrB, gupB), (pgB,))
            ob, obB = o_r.next()
            B.tt("dve", ob[:], cn[:], pg[:], ALU.mult, (cnB, pgB), (obB,))
            po_, poB_ = pst.next()
            for k in range(4):
                B.tr(po_[:, k * 128:(k + 1) * 128], ob[:, k * 128:(k + 1) * 128], identb[:], (obB, identB), (poB_,))
            mx, mxB = m_r.next()
            B.cp("act", mx[:].rearrange("p k t -> p (k t)"), po_[:, 0:512], (poB_,), (mxB,))
            B.store(MX[0:512, t0:t0 + 128].rearrange("(k p) t -> p k t", p=128), mx[:], (mxB,), (MXB,))

        stage1(*tiles[0])
        for p, (s, i) in enumerate(tiles):
            if p + 1 < len(tiles):
                stage1(*tiles[p + 1])
            if wgen is not None:
                next(wgen, None)
                next(wgen, None)
            stage2(s, i)
        if wgen is not None:
            for _ in wgen:
                pass
        S.flush()


_CACHE = {}


def kernel(**inputs):
    L, NS, NCORES = 4096, 2, 8
    xp = np.asarray(inputs["x_prompt"], np.float32)
    xs = np.asarray(inputs["x_sample"], np.float32)
    seqs = [xp[i] for i in range(xp.shape[0])] + [xs[i] for i in range(xs.shape[0])]
    nseq = len(seqs)
    zero = np.zeros((L, D), np.float32)
    slots = seqs + [zero] * (NCORES * NS - nseq)
    xs_list = [np.concatenate(slots[c * NS:(c + 1) * NS], axis=0) for c in range(NCORES)]
    if "nc" not in _CACHE:
        _CACHE["nc"] = build_program(L, NS)[0]
    nc = _CACHE["nc"]
    in_maps = make_inmaps(inputs, L, NS, NCORES, xs_list)
    res = run_bass_kernel_spmd(nc, in_maps, core_ids=list(range(NCORES)))
    outs = []
    for c in range(NCORES):
        yc = np.asarray(res.results[c]["y"], np.float32).reshape(NS, L, D)
        for s in range(NS):
            outs.append(yc[s])
    y_prompt = np.stack(outs[:xp.shape[0]], axis=0)
    y_sample = np.stack(outs[xp.shape[0]:nseq], axis=0)
    return (y_prompt, y_sample)
```
